# Optimizing a Trainium2 kernel written in Bass

```python
import math
import jax, jax.numpy as jnp
from jax import lax
import numpy as np

D_MODEL = 1024
BATCH = 1
SEQ = 16384
DEPTH = 4

N_EVEN = (DEPTH + 1) // 2
N_ODD = DEPTH // 2
A_GROUPS = 4
A_CHUNK = 128
A_WIDTH = D_MODEL // 2
A_GROUP_CH = A_WIDTH // A_GROUPS
B_HEADS = 8
B_HEAD_DIM = 64
B_WIDTH = B_HEADS * B_HEAD_DIM
Q_BLOCK = 128
MIX_WIDTH = A_WIDTH + B_WIDTH
IN_COLS = 2 * A_WIDTH + 3 * B_WIDTH + B_HEADS
S5_GROUP_CH = 16
S5_GROUPS = D_MODEL // S5_GROUP_CH
S5_STATE = 64
D_FF = 2816
CONV_W = 3
PLE_DIM = 256
EPS = 1e-6
NEG_INF = -1e30

kernel_name = "hybrid_gmlp_fox_s5_convffn_ple"


def rms_norm(x, g=None):
    xf = x.astype(jnp.float32)
    y = xf * lax.rsqrt(jnp.mean(xf * xf, axis=-1, keepdims=True) + EPS)
    if g is not None:
        y = y * g.astype(jnp.float32)
    return y.astype(x.dtype)


def gmlp_mixer(u, v, v_gain, w_s, b_s):
    bsz, seq = u.shape[0], u.shape[1]
    u = jax.nn.gelu(u)
    v = jax.nn.gelu(v).reshape(bsz, seq, A_GROUPS, A_GROUP_CH)
    v = rms_norm(v, v_gain.reshape(A_GROUPS, A_GROUP_CH))
    v = v.reshape(bsz, seq // A_CHUNK, A_CHUNK, A_GROUPS, A_GROUP_CH)
    tri = jnp.tril(jnp.ones((A_CHUNK, A_CHUNK), dtype=bool))
    w = jnp.where(tri[None], w_s, jnp.zeros_like(w_s))
    sv = jnp.einsum('gts,bnsgc->bntgc', w, v) + b_s.T[None, None, :, :, None]
    return u * sv.reshape(bsz, seq, A_WIDTH)


def fox_attention(q, k, v, f_logit, q_gain, k_gain):
    bsz, seq = q.shape[0], q.shape[1]
    q = rms_norm(q, q_gain)
    k = rms_norm(k, k_gain)
    c = jnp.cumsum(jax.nn.log_sigmoid(f_logit.astype(jnp.float32)), axis=1)
    nb = seq // Q_BLOCK
    qb = q.reshape(bsz, nb, Q_BLOCK, B_HEADS, B_HEAD_DIM).transpose(1, 0, 2, 3, 4)
    cb = c.reshape(bsz, nb, Q_BLOCK, B_HEADS).transpose(1, 0, 2, 3)
    pos_b = jnp.arange(seq, dtype=jnp.int32).reshape(nb, Q_BLOCK)
    kpos = jnp.arange(seq, dtype=jnp.int32)
    ck = c.transpose(0, 2, 1)
    scale = B_HEAD_DIM ** -0.5

    def block(args):
        qi, ci, pi = args
        s = jnp.einsum('bqhd,bkhd->bhqk', qi, k).astype(jnp.float32) * scale
        s = s + ci.transpose(0, 2, 1)[..., None] - ck[:, :, None, :]
        s = jnp.where(kpos[None, :] <= pi[:, None], s, NEG_INF)
        pr = jax.nn.softmax(s, axis=-1)
        return jnp.einsum('bhqk,bkhd->bqhd', pr.astype(v.dtype), v)

    o = lax.map(block, (qb, cb, pos_b))
    return o.transpose(1, 0, 2, 3, 4).reshape(bsz, seq, B_WIDTH)


def _complex_affine_combine(e1, e2):
    a1r, a1i, b1r, b1i = e1
    a2r, a2i, b2r, b2i = e2
    ar = a2r * a1r - a2i * a1i
    ai = a2r * a1i + a2i * a1r
    br = a2r * b1r - a2i * b1i + b2r
    bi = a2r * b1i + a2i * b1r + b2i
    return (ar, ai, br, bi)


def s5_mixer(u, a_re, a_im, log_dt, b_re, b_im, c_re, c_im, d):
    bsz, seq, _ = u.shape
    f32 = jnp.float32
    uf = u.astype(f32).reshape(bsz, seq, S5_GROUPS, S5_GROUP_CH)
    dt = jnp.exp(log_dt.astype(f32))[:, None]
    lr, li = a_re.astype(f32), a_im.astype(f32)
    mag = jnp.exp(lr * dt)
    ab_re, ab_im = mag * jnp.cos(li * dt), mag * jnp.sin(li * dt)
    den = lr * lr + li * li
    nr, ni = ab_re - 1.0, ab_im
    cr = (nr * lr + ni * li) / den
    ci = (ni * lr - nr * li) / den
    br, bi = b_re.astype(f32), b_im.astype(f32)
    bb_re = cr[..., None] * br - ci[..., None] * bi
    bb_im = cr[..., None] * bi + ci[..., None] * br
    bu_re = jnp.einsum('gpc,bsgc->bsgp', bb_re, uf)
    bu_im = jnp.einsum('gpc,bsgc->bsgp', bb_im, uf)
    a_r = jnp.broadcast_to(ab_re, bu_re.shape)
    a_i = jnp.broadcast_to(ab_im, bu_im.shape)
    _, _, xr, xi = lax.associative_scan(_complex_affine_combine, (a_r, a_i, bu_re, bu_im), axis=1)
    y = (jnp.einsum('gcp,bsgp->bsgc', c_re.astype(f32), xr)
         - jnp.einsum('gcp,bsgp->bsgc', c_im.astype(f32), xi)
         + d.astype(f32).reshape(S5_GROUPS, S5_GROUP_CH) * uf)
    return y.reshape(bsz, seq, D_MODEL).astype(u.dtype)


def conv_ffn(x, w_up, conv_w, conv_b, w_down):
    seq = x.shape[1]
    h = x @ w_up
    hp = jnp.pad(h, ((0, 0), (CONV_W - 1, 0), (0, 0)))
    hc = conv_b + conv_w[0] * hp[:, 0:seq]
    for j in range(1, CONV_W):
        hc = hc + conv_w[j] * hp[:, j:j + seq]
    g, up = jnp.split(hc, 2, axis=-1)
    return (jax.nn.silu(g) * up) @ w_down


def setup_inputs(seed: int = 0) -> dict:
    key = jax.random.key(seed)
    ks = jax.random.split(key, 32)
    f32 = jnp.float32

    def nrm(k, shape, scale):
        return scale * jax.random.normal(k, shape, f32)

    a_im_base = jnp.broadcast_to(math.pi * jnp.arange(S5_STATE, dtype=f32), (N_ODD, S5_GROUPS, S5_STATE))
    return {
        "x": nrm(ks[0], (BATCH, SEQ, D_MODEL), 1.0),
        "p": nrm(ks[1], (DEPTH, BATCH, SEQ, PLE_DIM), 1.0),
        "norm_mix": 1.0 + nrm(ks[2], (DEPTH, D_MODEL), 0.05),
        "norm_ffn": 1.0 + nrm(ks[3], (DEPTH, D_MODEL), 0.05),
        "ev_w_in": nrm(ks[4], (N_EVEN, D_MODEL, IN_COLS), D_MODEL ** -0.5),
        "ev_b_fgate": 4.0 + nrm(ks[5], (N_EVEN, B_HEADS), 0.5),
        "ev_q_norm": 1.0 + nrm(ks[6], (N_EVEN, B_HEAD_DIM), 0.05),
        "ev_k_norm": 1.0 + nrm(ks[7], (N_EVEN, B_HEAD_DIM), 0.05),
        "ev_v_norm": 1.0 + nrm(ks[8], (N_EVEN, A_WIDTH), 0.05),
        "ev_w_spatial": nrm(ks[9], (N_EVEN, A_GROUPS, A_CHUNK, A_CHUNK), A_CHUNK ** -0.5),
        "ev_b_spatial": 1.0 + nrm(ks[10], (N_EVEN, A_GROUPS, A_CHUNK), 0.1),
        "ev_w_out": nrm(ks[11], (N_EVEN, MIX_WIDTH, D_MODEL), MIX_WIDTH ** -0.5),
        "od_a_re": -0.5 + nrm(ks[12], (N_ODD, S5_GROUPS, S5_STATE), 0.01),
        "od_a_im": a_im_base + nrm(ks[13], (N_ODD, S5_GROUPS, S5_STATE), 0.01),
        "od_log_dt": jax.random.uniform(ks[14], (N_ODD, S5_GROUPS), f32, math.log(1e-3), math.log(1e-1)),
        "od_b_re": nrm(ks[15], (N_ODD, S5_GROUPS, S5_STATE, S5_GROUP_CH), (2 * S5_GROUP_CH) ** -0.5),
        "od_b_im": nrm(ks[16], (N_ODD, S5_GROUPS, S5_STATE, S5_GROUP_CH), (2 * S5_GROUP_CH) ** -0.5),
        "od_c_re": nrm(ks[17], (N_ODD, S5_GROUPS, S5_GROUP_CH, S5_STATE), S5_STATE ** -0.5),
        "od_c_im": nrm(ks[18], (N_ODD, S5_GROUPS, S5_GROUP_CH, S5_STATE), S5_STATE ** -0.5),
        "od_d": nrm(ks[19], (N_ODD, D_MODEL), 1.0),
        "od_w_glu": nrm(ks[20], (N_ODD, D_MODEL, 2 * D_MODEL), D_MODEL ** -0.5),
        "ffn_w_up": nrm(ks[21], (DEPTH, D_MODEL, 2 * D_FF), D_MODEL ** -0.5),
        "ffn_conv_w": nrm(ks[22], (DEPTH, CONV_W, 2 * D_FF), CONV_W ** -0.5),
        "ffn_conv_b": nrm(ks[23], (DEPTH, 2 * D_FF), 0.02),
        "ffn_w_down": nrm(ks[24], (DEPTH, D_FF, D_MODEL), D_FF ** -0.5),
        "ple_w_proj": nrm(ks[25], (DEPTH, PLE_DIM, D_MODEL), PLE_DIM ** -0.5),
        "ple_w_gate": nrm(ks[26], (DEPTH, D_MODEL, D_MODEL), D_MODEL ** -0.5),
    }


def reference(x, p, norm_mix, norm_ffn, ev_w_in, ev_b_fgate, ev_q_norm, ev_k_norm, ev_v_norm,
              ev_w_spatial, ev_b_spatial, ev_w_out, od_a_re, od_a_im, od_log_dt, od_b_re, od_b_im,
              od_c_re, od_c_im, od_d, od_w_glu, ffn_w_up, ffn_conv_w, ffn_conv_b, ffn_w_down,
              ple_w_proj, ple_w_gate):
    bsz, seq = x.shape[0], x.shape[1]
    splits = [A_WIDTH, 2 * A_WIDTH, 2 * A_WIDTH + B_WIDTH, 2 * A_WIDTH + 2 * B_WIDTH,
              2 * A_WIDTH + 3 * B_WIDTH]
    for i in range(DEPTH):
        h = rms_norm(x, norm_mix[i])
        if i % 2 == 0:
            e = i // 2
            z = h @ ev_w_in[e]
            u_a, v_a, q, k, v_b, f = jnp.split(z, splits, axis=-1)
            y_a = gmlp_mixer(u_a, v_a, ev_v_norm[e], ev_w_spatial[e], ev_b_spatial[e])
            shp = (bsz, seq, B_HEADS, B_HEAD_DIM)
            y_b = fox_attention(q.reshape(shp), k.reshape(shp), v_b.reshape(shp),
                                f + ev_b_fgate[e], ev_q_norm[e], ev_k_norm[e])
            x = x + jnp.concatenate([y_a, y_b], axis=-1) @ ev_w_out[e]
        else:
            o = i // 2
            y = s5_mixer(h, od_a_re[o], od_a_im[o], od_log_dt[o], od_b_re[o], od_b_im[o],
                         od_c_re[o], od_c_im[o], od_d[o])
            g_a, g_b = jnp.split(jax.nn.gelu(y) @ od_w_glu[o], 2, axis=-1)
            x = x + g_a * jax.nn.sigmoid(g_b)
        x = x + conv_ffn(rms_norm(x, norm_ffn[i]), ffn_w_up[i], ffn_conv_w[i], ffn_conv_b[i], ffn_w_down[i])
        gate = jax.nn.sigmoid(rms_norm(x) @ ple_w_gate[i])
        x = x + gate * (p[i] @ ple_w_proj[i])
    return x
```

```python
import numpy as np
import concourse.bass as bass
import concourse.mybir as mybir
from concourse.bass_utils import run_bass_kernel_spmd
from contextlib import ExitStack

F32 = mybir.dt.float32
BF16 = mybir.dt.bfloat16
ALU = mybir.AluOpType
AF = mybir.ActivationFunctionType
AX = mybir.AxisListType


class Buf:
    __slots__ = ("name", "w", "r")

    def __init__(self, name=""):
        self.name = name
        self.w = None
        self.r = []


class Op:
    __slots__ = ("eng", "fn", "deps", "signal", "sem", "val", "is_dma", "idx", "inc")

    def __init__(self, eng, fn, is_dma=False):
        self.eng = eng
        self.fn = fn
        self.deps = []
        self.signal = False
        self.sem = None
        self.val = 0
        self.is_dma = is_dma
        self.idx = 0
        self.inc = 16


ENGS = ("pe", "act", "dve", "pool", "sp")
N_DMA_SLOTS = 8


class Prog:
    def __init__(self, nc, strict_same_engine=True):
        self.nc = nc
        self.ops = {e: [] for e in ENGS}
        self.last = {e: None for e in ENGS}
        self.strict = strict_same_engine
        self.dma_slot_last = {}
        self.dma_slot_cnt = {}
        self.dma_rr = {e: 0 for e in ENGS}
        self.pending_bar = {e: [] for e in ENGS}
        self.n_ops = 0
        self.vals = {}
        self.eng_init = {}

    def _add(self, op, reads, writes):
        deps = []
        for b in reads:
            if b.w is not None:
                deps.append(b.w)
        for b in writes:
            if b.w is not None:
                deps.append(b.w)
            deps.extend(b.r)
        if self.pending_bar[op.eng]:
            deps.extend(self.pending_bar[op.eng])
            self.pending_bar[op.eng] = []
        seen = set()
        for d in deps:
            if d is op or id(d) in seen:
                continue
            seen.add(id(d))
            if (not d.is_dma) and d.eng == op.eng and (op.eng == "pe" or not self.strict):
                continue
            op.deps.append(d)
            d.signal = True
        for b in reads:
            if not op.is_dma:
                b.r = [x for x in b.r if x.is_dma or x.eng != op.eng]
            b.r.append(op)
        for b in writes:
            b.w = op
            b.r = []
        op.idx = len(self.ops[op.eng])
        self.ops[op.eng].append(op)
        self.last[op.eng] = op
        self.n_ops += 1
        return op

    def op(self, eng, fn, reads=(), writes=()):
        return self._add(Op(eng, fn), list(reads), list(writes))

    def dma(self, eng, fn, reads=(), writes=()):
        op = Op(eng, fn, is_dma=True)
        slot = self.dma_rr[eng]
        self.dma_rr[eng] = (slot + 1) % N_DMA_SLOTS
        key = (eng, slot)
        prev = self.dma_slot_last.get(key)
        op.sem = key
        self.dma_slot_cnt[key] = self.dma_slot_cnt.get(key, 0) + 16
        op.val = self.dma_slot_cnt[key]
        self._add(op, list(reads), list(writes))
        if prev is not None and prev not in op.deps:
            op.deps.append(prev)
        self.dma_slot_last[key] = op
        return op

    def coll(self, fn, reads=(), writes=()):
        op = Op("pool", fn, is_dma=True)
        op.inc = 1
        key = ("cc", 0)
        prev = self.dma_slot_last.get(key)
        op.sem = key
        self.dma_slot_cnt[key] = self.dma_slot_cnt.get(key, 0) + 1
        op.val = self.dma_slot_cnt[key]
        self._add(op, list(reads), list(writes))
        if prev is not None and prev not in op.deps:
            op.deps.append(prev)
        self.dma_slot_last[key] = op
        return op

    def barrier(self):
        lasts = [o for o in self.last.values() if o is not None]
        lasts += list(self.dma_slot_last.values())
        for e in ENGS:
            self.pending_bar[e] = list(lasts)

    def emit(self):
        nc = self.nc
        self.barrier()
        fin = Op("sp", None)
        self._add(fin, [], [])
        for e in ENGS:
            c = 0
            for o in self.ops[e]:
                if o.is_dma:
                    continue
                if o.signal:
                    c += 1
                    o.val = c
                    o.sem = ("eng", e)
            assert c < 1000000, (e, c)
        for k, v in self.dma_slot_cnt.items():
            assert v < 1000000, (k, v)
        with ExitStack() as es:
            sems = {}
            for e in ENGS:
                sems[("eng", e)] = es.enter_context(nc.semaphore("s_" + e))
            for key in self.dma_slot_cnt:
                sems[key] = es.enter_context(nc.semaphore("d_%s%d" % key))
            block = es.enter_context(nc.Block())

            def run(engname, handle):
                seen = {}
                init = getattr(self, "eng_init", {}).get(engname)
                if init is not None:
                    self.vals[engname] = init(handle)
                for o in self.ops[engname]:
                    need = {}
                    for d in o.deps:
                        if need.get(d.sem, 0) < d.val:
                            need[d.sem] = d.val
                    for s, v in need.items():
                        if seen.get(s, 0) >= v:
                            continue
                        handle.wait_ge(sems[s], v)
                        seen[s] = v
                    if o.fn is None:
                        continue
                    ins = o.fn(handle)
                    if o.is_dma:
                        ins.then_inc(sems[o.sem], o.inc)
                    elif o.signal:
                        ins.then_inc(sems[o.sem], 1)

            @block.tensor
            def _(h):
                run("pe", h)

            @block.scalar
            def _(h):
                run("act", h)

            @block.vector
            def _(h):
                run("dve", h)

            @block.gpsimd
            def _(h):
                run("pool", h)

            @block.sync
            def _(h):
                run("sp", h)


NCORES = 8
D = 1024
SEQ = 16384
TPC = SEQ // NCORES
NTT = TPC // 512
DFF = 2816
NJ = DFF // 128
EPS = 1e-6
import ml_dtypes
NPBF = ml_dtypes.bfloat16


import os
USE_WCACHE = os.environ.get('K_WCACHE', '1') == '1'
USE_POOLCONV = os.environ.get('K_POOLCONV', '0') == '1'
ARENA_BYTES = 212480


class Ctx:
    def __init__(self):
        self.nc = bass.Bass("TRN2", target_bir_lowering=False)
        self.es = ExitStack()
        self.P = Prog(self.nc)
        self.ps = self.es.enter_context(self.nc.psum_tensor("ps", [128, 8 * 512], F32))
        self.psb = [Buf("ps%d" % i) for i in range(8)]
        self.psi = 0
        self.nb = 0
        self.arena = self.es.enter_context(self.nc.sbuf_tensor("arena", [128, ARENA_BYTES], mybir.dt.uint8))
        self.off = 0
        self.peak = 0

    def sb(self, name, shape, dt=F32):
        n = 1
        for d_ in shape[1:]:
            n *= d_
        nbytes = n * mybir.dt.size(dt)
        off = (self.off + 63) // 64 * 64
        assert off + nbytes <= ARENA_BYTES, ("SBUF arena overflow", name, off, nbytes)
        self.off = off + nbytes
        self.peak = max(self.peak, self.off)
        ap = self.arena[0:shape[0], off:off + nbytes].bitcast(dt)
        if len(shape) == 3:
            ap = ap.rearrange("p (a b) -> p a b", a=shape[1])
        return ap

    def mark(self):
        return self.off

    def release(self, m):
        self.P.barrier()
        self.off = m

    def din(self, name, shape, dt=F32):
        return self.nc.dram_tensor(name, list(shape), dt, kind="ExternalInput").ap()

    def dout(self, name, shape, dt=F32):
        return self.nc.dram_tensor(name, list(shape), dt, kind="ExternalOutput").ap()

    def dint(self, name, shape, dt=F32):
        return self.nc.dram_tensor(name, list(shape), dt).ap()

    def wcache_get(self, key):
        if not hasattr(self, "wkeys"):
            self.wkeys = {}
            self.wcache = self.dint("wcache", [160, 128, 1408], BF16)
        if key not in self.wkeys:
            idx = len(self.wkeys)
            assert idx < 160
            self.wkeys[key] = (self.wcache[idx], self.buf("wc"))
        return self.wkeys[key]

    def bank(self):
        i = self.psi
        self.psi = (i + 1) % 8
        return self.ps[:, i * 512:(i + 1) * 512], self.psb[i]

    def buf(self, name=""):
        self.nb += 1
        return Buf(name + str(self.nb))

    def finish(self):
        self.P.emit()
        self.es.close()
        return self.nc


class WLoader:
    def __init__(self, C, maxelems, nbuf=2, cast_eng="pool", dma_eng="sp"):
        self.C = C
        self.st = [C.sb("wst", [128, maxelems], F32) for i in range(nbuf)]
        self.sbuf = [C.buf("wst") for _ in range(nbuf)]
        self.i = 0
        self.n = nbuf
        self.maxelems = maxelems
        self.cast_eng = cast_eng
        self.dma_eng = dma_eng

    def load(self, src, dst, dstbuf, K, M, key=None, first=True):
        C = self.C
        P = C.P
        assert K * M <= self.maxelems
        if not USE_WCACHE:
            key = None
        if key is not None:
            cap, cB = C.wcache_get(key)
            cview = cap[:, 0:K * M].rearrange("p (k m) -> p k m", k=K)
            if not first:
                P.dma(self.dma_eng, lambda e: e.dma_start(out=dst, in_=cview), [cB], [dstbuf])
                return
        i = self.i
        self.i = (i + 1) % self.n
        st = self.st[i][:, 0:K * M].rearrange("p (k m) -> p k m", k=K)
        P.dma(self.dma_eng, lambda e: e.dma_start(out=st, in_=src), [], [self.sbuf[i]])
        P.op(self.cast_eng, lambda e: e.tensor_copy(dst, st), [self.sbuf[i]], [dstbuf])
        if key is not None:
            P.dma("act", lambda e: e.dma_start(out=cview, in_=dst), [dstbuf], [cB])


def load_f32(C, dst, src, buf, eng="sp"):
    C.P.dma(eng, lambda e: e.dma_start(out=dst, in_=src), [], [buf])


class Norm:
    def __init__(self, C, ones_bf, onesB, epscol, epsB):
        self.C = C
        self.sq = [C.sb("nsq%d" % i, [128, 512], BF16) for i in range(2)]
        self.sqB = [C.buf("nsq") for _ in range(2)]
        self.i = 0
        self.ones = ones_bf
        self.onesB = onesB
        self.eps = epscol
        self.epsB = epsB

    def rstd(self, xk, xbufs, n, out, outB, nk=8, mean_div=1024.0, ones=None):
        C, P = self.C, self.C.P
        bank, bb = C.bank()
        ones = self.ones if ones is None else ones
        np_ = out.shape[0]
        for k in range(nk):
            i = self.i
            self.i ^= 1
            sq = self.sq[i][0:np_, 0:n]
            src = xk(k)
            P.op("act", lambda e, sq=sq, src=src: e.activation(out=sq, in_=src, func=AF.Square), xbufs, [self.sqB[i]])
            P.op("pe", lambda e, sq=sq, k=k: e.matmul(bank[0:np_, 0:n], ones[0:np_, 0:np_], sq, start=(k == 0), stop=(k == nk - 1)),
                 [self.sqB[i], self.onesB], [bb])
        P.op("act", lambda e: e.activation(out=out, in_=bank[0:np_, 0:n], func=AF.Sqrt, scale=1.0 / mean_div, bias=self.eps[0:np_, 0:1]),
             [bb, self.epsB], [outB])
        P.op("dve", lambda e: e.reciprocal(out, out), [outB], [outB])


def consts(C):
    P = C.P
    ones = C.sb("ones_bf", [128, 128], BF16)
    onesB = C.buf("ones")
    P.op("pool", lambda e: e.memset(ones[:], 1.0), [], [onesB])
    eps = C.sb("epscol", [128, 1], F32)
    epsB = C.buf("eps")
    P.op("pool", lambda e: e.memset(eps[:], EPS), [], [epsB])
    return ones, onesB, eps, epsB


def emit_post(C, mode, io):
    P = C.P
    m0 = C.mark()
    dbg = 3
    TH = TPC + 2
    wmix = io["wmix"]
    gffn_d = io["gffn"]; wup = io["wup"]; cw_d = io["cw"]; cb_d = io["cb"]; wdn = io["wdn"]
    wgate = io["wgate"]; wproj = io["wproj"]; pT_d = io["pT"]; gnext_d = io["gnext"]; hmask_d = io["hmask"]
    xout = io.get("xout"); hn_d = io.get("hn_d")
    want_hn = hn_d is not None
    outB = io["outB"]; hnB_d = io["hn_dB"]

    ones, onesB, eps, epsB = C.K
    NRM = Norm(C, ones, onesB, eps, epsB)
    WL = WLoader(C, 11 * 128, nbuf=2)

    xs = io["xs"]; xsB = io["xsB"]; xsH = io["xsH"]
    ym = C.sb("ym", [128, 8, TH], BF16)
    ymB = C.buf("ym")
    smallB = C.buf("small")
    gffn = C.sb("gffn", [128, 8]); cw = C.sb("cw", [128, 3, 2 * NJ]); cb = C.sb("cb", [128, 2 * NJ])
    gnext = C.sb("gnext", [128, 8]); hmask = C.sb("hmask", [128, 1])
    for dst, src in ((gffn, gffn_d), (cw, cw_d), (cb, cb_d), (gnext, gnext_d), (hmask, hmask_d)):
        load_f32(C, dst[:], src, smallB, "pool")
    if mode == "odd":
        gmix = C.sb("gmix", [128, 8]); dvec = C.sb("dvec", [128, 8]); gd = C.sb("gd", [128, 8])
        gdB = C.buf("gd")
        load_f32(C, gmix[:], io["gmix"], smallB, "pool")
        load_f32(C, dvec[:], io["dvec"], smallB, "pool")
        P.op("dve", lambda e: e.tensor_tensor(out=gd[:], in0=gmix[:], in1=dvec[:], op=ALU.mult), [smallB], [gdB])
    hal = C.sb("hal", [128, 24]); halB = C.buf("hal")
    hrecv = io["hrecv"]; hrecvB = io["hrecvB"]
    P.dma("act", lambda e: e.dma_start(out=hal[:, :], in_=hrecv[bass.ds(P.vals["act"]["pid"] * 128, 128), :]), [hrecvB], [halB])
    P.op("dve", lambda e: e.tensor_copy(xs[:, :, 0:2], hal[:, 0:16].rearrange("p (k t) -> p k t", k=8)), [halB], [xsH])
    cidv = lambda: P.vals["act"]["cid"]
    if mode == "even":
        ya_d = io["ya_d"]; og = io["og"]
        P.dma("sp", lambda e: e.dma_start(out=ym[:, 0:4, 2:TH], in_=ya_d), [io["ya_dB"]], [ymB])
        P.op("dve", lambda e: e.tensor_copy(ym[:, 0:4, 0:2], hal[:, 16:24].rearrange("p (k t) -> p k t", k=4)), [halB], [ymB])
        P.dma("act", lambda e: e.dma_start(out=ym[:, 4:8, :], in_=og.rearrange("(m p) t -> p m t", p=128)[:, :, bass.ds(cidv() * TPC, TH)]), [io["ogB"]], [ymB])
    else:
        ycg = io["ycg"]
        P.dma("act", lambda e: e.dma_start(out=ym[:, :, :], in_=ycg.rearrange("(k p) t -> p k t", p=128)[:, :, bass.ds(cidv() * TPC, TH)]), [io["ycgB"]], [ymB])

    def xbuf(c0):
        return xsH if c0 == 0 else xsB[(c0 - 2) // 512]

    NWT = 6
    wt = [C.sb("wt%d" % i, [128, 8, 128], BF16) for i in range(NWT)]
    wtB = [C.buf("wt") for _ in range(NWT)]
    wti = [0]

    def next_wt():
        i = wti[0]
        wti[0] = (i + 1) % NWT
        return wt[i], wtB[i]

    wd = [C.sb("wd%d" % i, [128, NJ, 128], BF16) for i in range(2)]
    wdB = [[C.buf("wd") for _ in range(2)] for _ in range(2)]
    rstd = C.sb("rstd", [128, 512], F32); rstdB = C.buf("rstd")
    xh = [C.sb("xh%d" % i, [128, 8, 514], BF16) for i in range(2)]
    xhB = [C.buf("xh") for _ in range(2)]
    xhH = [C.buf("xhH") for _ in range(2)]
    aT = C.sb("aT", [128, NJ, 512], BF16); aTB = [C.buf("aT") for _ in range(NJ)]
    hsb = [C.sb("hsb%d" % i, [128, 514], F32) for i in range(2)]; hsbB = [C.buf("hsb") for _ in range(2)]
    acc = [C.sb("acc%d" % i, [128, 512], F32) for i in range(2)]; accB = [C.buf("acc") for _ in range(2)]
    sg = C.sb("sg", [128, 512], F32); sgB = C.buf("sg")
    tmpP = C.sb("tmpP", [128, 512], F32); tmpPB = C.buf("tmpP")
    pT = C.sb("pT", [128, 2, 512], BF16); pTB = C.buf("pT")
    wpj = [C.sb("wpj%d" % i, [128, 2, 128], BF16) for i in range(2)]; wpjB = [C.buf("wpj") for _ in range(2)]
    if mode == "odd":
        ygt = C.sb("ygt", [128, 8, 514], BF16); ygB = C.buf("yg")
        tmpf = C.sb("tmpf", [128, 512], F32); tmpB = C.buf("tmpf")
    hnt = C.sb("hnt", [128, 8, 512], BF16); hnB = C.buf("hnt")

    def wsrc(w, c0, ncol=128, K=8):
        return w.rearrange("(k p) c -> p k c", p=128)[:, :, c0:c0 + ncol]

    def mixer(tt):
        cols = [(2 + tt * 512, 512)]
        if tt == 0:
            cols = [(0, 2)] + cols
        if mode == "even":
            for fo in range(8):
                w, wB = next_wt()
                WL.load(wsrc(wmix, fo * 128), w[:], wB, 8, 128, key=("mix", fo), first=(tt == 0))
                for (c0, n) in cols:
                    bank, bb = C.bank()
                    for k in range(8):
                        P.op("pe", lambda e, k=k, c0=c0, n=n, bank=bank, w=w: e.matmul(bank[:, 0:n], w[:, k, :], ym[:, k, c0:c0 + n], start=(k == 0), stop=(k == 7)),
                             [wB, ymB], [bb])
                    xb = xbuf(c0)
                    P.op("dve", lambda e, fo=fo, c0=c0, n=n, bank=bank: e.tensor_tensor(out=xs[:, fo, c0:c0 + n], in0=xs[:, fo, c0:c0 + n], in1=bank[:, 0:n], op=ALU.add),
                         [bb, xb], [xb])
        else:
            for (c0, n) in cols:
                yc0 = 0 if c0 == 0 else 2
                xb = xbuf(c0)
                NRM.rstd(lambda k, c0=c0, n=n: xs[:, k, c0:c0 + n], [xb], n, rstd[:, 0:n], rstdB)
                for k in range(8):
                    P.op("dve", lambda e, k=k, c0=c0, n=n: e.scalar_tensor_tensor(out=tmpf[:, 0:n], in0=xs[:, k, c0:c0 + n], scalar=gd[:, k:k + 1], in1=rstd[:, 0:n], op0=ALU.mult, op1=ALU.mult),
                         [xb, gdB, rstdB], [tmpB])
                    P.op("dve", lambda e, k=k, c0=c0, n=n: e.tensor_tensor(out=tmpf[:, 0:n], in0=tmpf[:, 0:n], in1=ym[:, k, c0:c0 + n], op=ALU.add),
                         [tmpB, ymB], [tmpB])
                    P.op("act", lambda e, k=k, n=n, yc0=yc0: e.activation(out=ygt[:, k, yc0:yc0 + n], in_=tmpf[:, 0:n], func=AF.Gelu), [tmpB], [ygB])
            for fo in range(8):
                wa, waB = next_wt()
                WL.load(wsrc(wmix, fo * 128), wa[:], waB, 8, 128, key=("mixa", fo), first=(tt == 0))
                wb, wbB = next_wt()
                WL.load(wsrc(wmix, D + fo * 128), wb[:], wbB, 8, 128, key=("mixb", fo), first=(tt == 0))
                for (c0, n) in cols:
                    yc0 = 0 if c0 == 0 else 2
                    ba, bab = C.bank()
                    bg, bgb = C.bank()
                    for k in range(8):
                        P.op("pe", lambda e, k=k, n=n, yc0=yc0, ba=ba, wa=wa: e.matmul(ba[:, 0:n], wa[:, k, :], ygt[:, k, yc0:yc0 + n], start=(k == 0), stop=(k == 7)), [waB, ygB], [bab])
                    for k in range(8):
                        P.op("pe", lambda e, k=k, n=n, yc0=yc0, bg=bg, wb=wb: e.matmul(bg[:, 0:n], wb[:, k, :], ygt[:, k, yc0:yc0 + n], start=(k == 0), stop=(k == 7)), [wbB, ygB], [bgb])
                    P.op("act", lambda e, n=n, bg=bg: e.activation(out=sg[:, 0:n], in_=bg[:, 0:n], func=AF.Sigmoid), [bgb], [sgB])
                    P.op("dve", lambda e, n=n, ba=ba: e.tensor_tensor(out=sg[:, 0:n], in0=sg[:, 0:n], in1=ba[:, 0:n], op=ALU.mult), [sgB, bab], [sgB])
                    xb = xbuf(c0)
                    P.op("dve", lambda e, fo=fo, c0=c0, n=n: e.tensor_tensor(out=xs[:, fo, c0:c0 + n], in0=xs[:, fo, c0:c0 + n], in1=sg[:, 0:n], op=ALU.add), [sgB, xb], [xb])

    def gnorm(c0, n, gain, gainB, dst_fn, dstB):
        xb = xbuf(c0)
        NRM.rstd(lambda k: xs[:, k, c0:c0 + n], [xb], n, rstd[:, 0:n], rstdB)
        for k in range(8):
            if gain is None:
                P.op("dve", lambda e, k=k: e.tensor_tensor(out=dst_fn(k), in0=xs[:, k, c0:c0 + n], in1=rstd[:, 0:n], op=ALU.mult), [xb, rstdB], [dstB])
            else:
                P.op("dve", lambda e, k=k: e.scalar_tensor_tensor(out=dst_fn(k), in0=xs[:, k, c0:c0 + n], scalar=gain[:, k:k + 1], in1=rstd[:, 0:n], op0=ALU.mult, op1=ALU.mult),
                     [xb, rstdB, gainB], [dstB])

    def tile_pass(tt):
        c0 = 2 + tt * 512
        xi = tt % 2
        X = xh[xi]
        mixer(tt)
        if dbg < 2:
            P.dma("pool", lambda e, tt=tt, c0=c0: e.dma_start(out=xout[:, :, tt * 512:(tt + 1) * 512], in_=xs[:, :, c0:c0 + 512]), [xsB[tt]], [outB])
            return
        if tt == 0:
            gnorm(0, 2, gffn, smallB, lambda k: X[:, k, 0:2], xhH[xi])
        gnorm(c0, 512, gffn, smallB, lambda k: X[:, k, 2:514], xhB[xi])
        for j in range(NJ):
            tiles = []
            for half in range(2):
                w, wB = next_wt()
                WL.load(wsrc(wup, half * DFF + j * 128), w[:], wB, 8, 128, key=("up", half, j), first=(tt == 0))
                bank, bb = C.bank()
                hb, hbb = C.bank()
                for k in range(8):
                    P.op("pe", lambda e, k=k, bank=bank, w=w: e.matmul(bank[:, :], w[:, k, :], X[:, k, 2:514], start=(k == 0), stop=(k == 7)), [wB, xhB[xi]], [bb])
                for k in range(8):
                    P.op("pe", lambda e, k=k, hb=hb, w=w: e.matmul(hb[:, 0:2], w[:, k, :], X[:, k, 0:2], start=(k == 0), stop=(k == 7)), [wB, xhH[xi]], [hbb])
                n_ = half * NJ + j
                H = hsb[half]
                A = acc[half]
                P.op("act", lambda e, H=H, bank=bank: e.activation(out=H[:, 2:514], in_=bank[:, :], func=AF.Copy), [bb], [hsbB[half]])
                if tt == 0:
                    P.op("act", lambda e, H=H, hb=hb: e.activation(out=H[:, 0:2], in_=hb[:, 0:2], func=AF.Identity, scale=hmask[:, 0:1]), [hbb, smallB], [hsbB[half]])
                else:
                    P.op("act", lambda e, H=H, hb=hb: e.activation(out=H[:, 0:2], in_=hb[:, 0:2], func=AF.Copy), [hbb], [hsbB[half]])
                P.op("act", lambda e, H=H, A=A, n_=n_: e.activation(out=A[:, :], in_=H[:, 2:514], func=AF.Identity, scale=cw[:, 2, n_:n_ + 1], bias=cb[:, n_:n_ + 1]),
                     [hsbB[half], smallB], [accB[half]])
                if USE_POOLCONV and half == 1 and tt > 0:
                    for tap in (1, 0):
                        P.op("pool", lambda e, H=H, n_=n_, tap=tap: e.tensor_scalar(out=tmpP[:, :], in0=H[:, tap:tap + 512], scalar1=cw[:, tap, n_:n_ + 1], scalar2=None, op0=ALU.mult),
                             [hsbB[half], smallB], [tmpPB])
                        P.op("pool", lambda e, A=A: e.tensor_tensor(out=A[:, :], in0=A[:, :], in1=tmpP[:, :], op=ALU.add), [tmpPB, accB[half]], [accB[half]])
                else:
                    P.op("dve", lambda e, H=H, A=A, n_=n_: e.scalar_tensor_tensor(out=A[:, :], in0=H[:, 1:513], scalar=cw[:, 1, n_:n_ + 1], in1=A[:, :], op0=ALU.mult, op1=ALU.add),
                         [hsbB[half], smallB, accB[half]], [accB[half]])
                    P.op("dve", lambda e, H=H, A=A, n_=n_: e.scalar_tensor_tensor(out=A[:, :], in0=H[:, 0:512], scalar=cw[:, 0, n_:n_ + 1], in1=A[:, :], op0=ALU.mult, op1=ALU.add),
                         [hsbB[half], smallB, accB[half]], [accB[half]])
            P.op("act", lambda e: e.activation(out=acc[0][:, :], in_=acc[0][:, :], func=AF.Silu), [accB[0]], [accB[0]])
            P.op("dve", lambda e, j=j: e.tensor_tensor(out=aT[:, j, :], in0=acc[0][:, :], in1=acc[1][:, :], op=ALU.mult), [accB[0], accB[1]], [aTB[j]])
        for fo in range(8):
            di = fo % 2
            for hh in range(2):
                WL.load(wdn.rearrange("(j p) c -> p j c", p=128)[:, hh * 11:(hh + 1) * 11, fo * 128:(fo + 1) * 128], wd[di][:, hh * 11:(hh + 1) * 11, :], wdB[di][hh], 11, 128, key=("dn", fo, hh), first=(tt == 0))
            bank, bb = C.bank()
            for j in range(NJ):
                P.op("pe", lambda e, j=j, bank=bank, di=di: e.matmul(bank[:, :], wd[di][:, j, :], aT[:, j, :], start=(j == 0), stop=(j == NJ - 1)), [wdB[di][j // 11], aTB[j]], [bb])
            P.op("dve", lambda e, fo=fo, bank=bank: e.tensor_tensor(out=xs[:, fo, c0:c0 + 512], in0=xs[:, fo, c0:c0 + 512], in1=bank[:, :], op=ALU.add), [bb, xsB[tt]], [xsB[tt]])
        if dbg < 3:
            P.dma("pool", lambda e, tt=tt, c0=c0: e.dma_start(out=xout[:, :, tt * 512:(tt + 1) * 512], in_=xs[:, :, c0:c0 + 512]), [xsB[tt]], [outB])
            return
        X3 = xh[xi]
        if tt + 1 < NTT:
            Xn = xh[1 - xi]
            P.op("pool", lambda e, X=X, Xn=Xn: e.tensor_copy(Xn[:, :, 0:2], X[:, :, 512:514]), [xhB[xi]], [xhH[1 - xi]])
        gnorm(c0, 512, None, None, lambda k: X3[:, k, 2:514], xhB[xi])
        WL.load(pT_d[:, :, tt * 512:(tt + 1) * 512], pT[:], pTB, 2, 512)
        for fo in range(8):
            w, wB = next_wt()
            WL.load(wsrc(wgate, fo * 128), w[:], wB, 8, 128, key=("gate", fo), first=(tt == 0))
            pi = fo % 2
            WL.load(wproj.rearrange("(k p) c -> p k c", p=128)[:, :, fo * 128:(fo + 1) * 128], wpj[pi][:], wpjB[pi], 2, 128, key=("proj", fo), first=(tt == 0))
            bg, bgb = C.bank()
            bp, bpb = C.bank()
            for k in range(8):
                P.op("pe", lambda e, k=k, bg=bg, w=w: e.matmul(bg[:, :], w[:, k, :], X3[:, k, 2:514], start=(k == 0), stop=(k == 7)), [wB, xhB[xi]], [bgb])
            for k in range(2):
                P.op("pe", lambda e, k=k, bp=bp, pi=pi: e.matmul(bp[:, :], wpj[pi][:, k, :], pT[:, k, :], start=(k == 0), stop=(k == 1)), [wpjB[pi], pTB], [bpb])
            P.op("act", lambda e, bg=bg: e.activation(out=sg[:, :], in_=bg[:, :], func=AF.Sigmoid), [bgb], [sgB])
            P.op("dve", lambda e, bp=bp: e.tensor_tensor(out=sg[:, :], in0=sg[:, :], in1=bp[:, :], op=ALU.mult), [sgB, bpb], [sgB])
            P.op("dve", lambda e, fo=fo: e.tensor_tensor(out=xs[:, fo, c0:c0 + 512], in0=xs[:, fo, c0:c0 + 512], in1=sg[:, :], op=ALU.add), [sgB, xsB[tt]], [xsB[tt]])
        if xout is not None:
            P.dma("pool", lambda e, tt=tt, c0=c0: e.dma_start(out=xout[:, :, tt * 512:(tt + 1) * 512], in_=xs[:, :, c0:c0 + 512]), [xsB[tt]], [outB])
        if want_hn:
            gnorm(c0, 512, gnext, smallB, lambda k: hnt[:, k, :], hnB)
            P.dma("pool", lambda e, tt=tt: e.dma_start(out=hn_d.rearrange("(k p) t -> p k t", p=128)[:, :, tt * 512:(tt + 1) * 512], in_=hnt[:, :, :]), [hnB], [hnB_d])
    for tt in range(NTT):
        tile_pass(tt)
    C.release(m0)


def emit_pre_even(C, io):
    P = C.P
    m0 = C.mark()
    hn_d = io["hn_d"]
    win = io["win"]; vgain_d = io["vgain"]; wsT_d = io["wsT"]; tri_d = io["tri"]; bsb_d = io["bsb"]
    qg_d = io["qg"]; kg_d = io["kg"]; bf_d = io["bfg"]
    ya_d = io["ya_d"]; qks = io["qks"]; Vs = io["Vs"]; lss = io["lss"]
    xs = io["xs"]; xsB = io["xsB"]

    ones, onesB, eps, epsB = C.K
    NRM = Norm(C, ones, onesB, eps, epsB)
    WL = WLoader(C, 8 * 128, nbuf=3)
    onef = C.sb("onef", [128, 1]); onefB = C.buf("onef")
    P.op("pool", lambda e: e.memset(onef[:], 1.0), [], [onefB])

    X = C.sb("X", [128, 8, TPC], BF16); XB = C.buf("X")
    for k in range(8):
        P.dma("sp", lambda e, k=k: e.dma_start(out=X[:, k, :], in_=hn_d[k * 128:(k + 1) * 128, :]), [io["hn_dB"]], [XB])
    smallB = C.buf("small")
    vgain = C.sb("vgain", [128, 512]); wsT = C.sb("wsT", [128, 4, 128]); tri = C.sb("tri", [128, 128]); bsb = C.sb("bsb", [128, 512])
    qg = C.sb("qg", [64, 1]); kg = C.sb("kg", [64, 1]); bfg = C.sb("bfg", [8, 1])
    for dst, src in ((vgain, vgain_d), (wsT, wsT_d), (tri, tri_d), (bsb, bsb_d), (qg, qg_d), (kg, kg_d), (bfg, bf_d)):
        load_f32(C, dst[:], src, smallB, "pool")
    wsb = C.sb("wsb", [128, 4, 128], BF16); wsbB = C.buf("wsb")
    P.op("dve", lambda e: e.tensor_tensor(out=wsb[:], in0=wsT[:], in1=tri[:, :].unsqueeze(1).to_broadcast([128, 4, 128]), op=ALU.mult), [smallB], [wsbB])
    qgs = C.sb("qgs", [64, 1]); negb = C.sb("negb", [8, 1]); sm2B = C.buf("sm2")
    P.op("dve", lambda e: e.tensor_scalar(out=qgs[:], in0=qg[:], scalar1=0.125, scalar2=None, op0=ALU.mult), [smallB], [sm2B])
    P.op("dve", lambda e: e.tensor_scalar(out=negb[:], in0=bfg[:], scalar1=-1.0, scalar2=None, op0=ALU.mult), [smallB], [sm2B])

    wt = [C.sb("wt%d" % i, [128, 8, 128], BF16) for i in range(3)]
    wtB = [C.buf("wt") for _ in range(3)]
    wti = [0]

    def next_wt():
        i = wti[0]
        wti[0] = (i + 1) % 3
        return wt[i], wtB[i]

    def wsrc(c0, ncol=128):
        return win.rearrange("(k p) c -> p k c", p=128)[:, :, c0:c0 + ncol]

    gu = C.sb("gu", [128, 4, TPC], BF16); guB = [C.buf("gu") for _ in range(4)]
    for ft in range(4):
        w, wB = next_wt()
        WL.load(wsrc(ft * 128), w[:], wB, 8, 128)
        for tt in range(NTT):
            bank, bb = C.bank()
            for k in range(8):
                P.op("pe", lambda e, k=k, bank=bank, w=w, tt=tt: e.matmul(bank[:, :], w[:, k, :], X[:, k, tt * 512:(tt + 1) * 512], start=(k == 0), stop=(k == 7)), [wB, XB], [bb])
            P.op("act", lambda e, bank=bank, ft=ft, tt=tt: e.activation(out=gu[:, ft, tt * 512:(tt + 1) * 512], in_=bank[:, :], func=AF.Gelu), [bb], [guB[ft]])

    wv = C.sb("wv", [128, 8, 512], BF16); wvB = [C.buf("wv") for _ in range(4)]

    def load_wv(col0):
        for c in range(4):
            WL.load(wsrc(col0 + c * 128), wv[:, :, c * 128:(c + 1) * 128], wvB[c], 8, 128)

    load_wv(512)
    gv = C.sb("gv", [128, 512]); gvB = C.buf("gv")
    sqv = C.sb("sqv", [128, 512]); sqvB = C.buf("sqv")
    ss = C.sb("ss", [128, 4]); ssB = C.buf("ss")
    vn = C.sb("vn", [128, 512], BF16); vnB = C.buf("vn")
    tmp = C.sb("tmp", [128, 512]); tmpB = C.buf("tmp")
    ya = C.sb("ya", [128, 4, TPC], BF16); yaB = C.buf("ya")
    for c in range(TPC // 128):
        bank, bb = C.bank()
        for k in range(8):
            P.op("pe", lambda e, k=k, bank=bank, c=c: e.matmul(bank[:, :], X[:, k, c * 128:(c + 1) * 128], wv[:, k, :], start=(k == 0), stop=(k == 7)), wvB + [XB], [bb])
        P.op("act", lambda e, bank=bank: e.activation(out=gv[:], in_=bank[:, :], func=AF.Gelu), [bb], [gvB])
        P.op("act", lambda e: e.activation(out=sqv[:], in_=gv[:], func=AF.Square), [gvB], [sqvB])
        P.op("dve", lambda e: e.tensor_reduce(out=ss[:, :], in_=sqv[:, :].rearrange("p (g c) -> p g c", g=4), axis=AX.X, op=ALU.add), [sqvB], [ssB])
        P.op("act", lambda e: e.activation(out=ss[:, :], in_=ss[:, :], func=AF.Sqrt, scale=1.0 / 128, bias=eps[:, 0:1]), [ssB, epsB], [ssB])
        P.op("dve", lambda e: e.reciprocal(ss[:, :], ss[:, :]), [ssB], [ssB])
        for g in range(4):
            P.op("dve", lambda e, g=g: e.scalar_tensor_tensor(out=vn[:, g * 128:(g + 1) * 128], in0=gv[:, g * 128:(g + 1) * 128], scalar=ss[:, g:g + 1], in1=vgain[:, g * 128:(g + 1) * 128], op0=ALU.mult, op1=ALU.mult),
                 [gvB, ssB, smallB], [vnB])
        b2, b2b = C.bank()
        for g in range(4):
            P.op("pe", lambda e, g=g, b2=b2: e.matmul(b2[:, g * 128:(g + 1) * 128], vn[:, g * 128:(g + 1) * 128], wsb[:, g, :], start=True, stop=True), [vnB, wsbB], [b2b])
        P.op("dve", lambda e, b2=b2: e.tensor_tensor(out=tmp[:], in0=b2[:, :], in1=bsb[:], op=ALU.add), [b2b, smallB], [tmpB])
        P.op("dve", lambda e, c=c: e.tensor_tensor(out=ya[:, :, c * 128:(c + 1) * 128], in0=tmp[:, :].rearrange("p (g c) -> p g c", g=4), in1=gu[:, :, c * 128:(c + 1) * 128], op=ALU.mult),
             [tmpB] + guB, [yaB])
    for ft in range(4):
        P.dma("pool", lambda e, ft=ft: e.dma_start(out=ya_d[:, ft, :], in_=ya[:, ft, :]), [yaB], [io["ya_dB"]])
    hs = C.sb("hs", [128, 24]); hsB = C.buf("hs")
    P.op("dve", lambda e: e.tensor_copy(hs[:, 0:16].rearrange("p (k t) -> p k t", k=8), xs[:, :, TPC:TPC + 2]), [xsB[NTT - 1]], [hsB])
    P.op("dve", lambda e: e.tensor_copy(hs[:, 16:24].rearrange("p (k t) -> p k t", k=4), ya[:, :, TPC - 2:TPC]), [yaB, hsB], [hsB])
    P.dma("pool", lambda e: e.dma_start(out=io["hsend"], in_=hs[:, :]), [hsB], [io["hsendB"]])

    qs = C.sb("qs", [64, 512]); qsB = C.buf("qs")
    rq = C.sb("rq", [64, 512]); rqB = C.buf("rq")
    qo = [C.sb("qo%d" % i, [64, 512], BF16) for i in range(2)]; qoB = [C.buf("qo") for _ in range(2)]
    cnt = 0
    for (col0, gcol, roff) in ((1024, qgs, 0), (1536, kg, 64)):
        for h2 in range(4):
            w, wB = next_wt()
            WL.load(wsrc(col0 + h2 * 128), w[:], wB, 8, 128)
            for hh in range(2):
                h = 2 * h2 + hh
                for tt in range(NTT):
                    bank, bb = C.bank()
                    for k in range(8):
                        P.op("pe", lambda e, k=k, bank=bank, w=w, tt=tt, hh=hh: e.matmul(bank[0:64, :], w[:, k, hh * 64:(hh + 1) * 64], X[:, k, tt * 512:(tt + 1) * 512], start=(k == 0), stop=(k == 7)), [wB, XB], [bb])
                    P.op("act", lambda e, bank=bank: e.activation(out=qs[:, :], in_=bank[0:64, :], func=AF.Copy), [bb], [qsB])
                    NRM.rstd(lambda k: qs[:, :], [qsB], 512, rq[:, :], rqB, nk=1, mean_div=64.0)
                    qi = cnt % 2
                    cnt += 1
                    P.op("dve", lambda e, qi=qi, gcol=gcol: e.scalar_tensor_tensor(out=qo[qi][:, :], in0=qs[:, :], scalar=gcol[:, 0:1], in1=rq[:, :], op0=ALU.mult, op1=ALU.mult),
                         [qsB, rqB, smallB, sm2B], [qoB[qi]])
                    P.dma("pool", lambda e, qi=qi, roff=roff, h=h, tt=tt: e.dma_start(out=qks[h * 128 + roff:h * 128 + roff + 64, tt * 512:(tt + 1) * 512], in_=qo[qi][:, :]), [qoB[qi]], [io["qksB"]])

    w, wB = next_wt()
    WL.load(wsrc(2560, 8), w[:, :, 0:8], wB, 8, 8)
    fe = C.sb("fe", [8, 512]); feB = C.buf("fe")
    for tt in range(NTT):
        bank, bb = C.bank()
        for k in range(8):
            P.op("pe", lambda e, k=k, bank=bank, w=w, tt=tt: e.matmul(bank[0:8, :], w[:, k, 0:8], X[:, k, tt * 512:(tt + 1) * 512], start=(k == 0), stop=(k == 7)), [wB, XB], [bb])
        P.op("act", lambda e, bank=bank: e.activation(out=fe[:, :], in_=bank[0:8, :], func=AF.Exp, scale=-1.0, bias=negb[:, 0:1]), [bb, sm2B], [feB])
        P.op("act", lambda e: e.activation(out=fe[:, :], in_=fe[:, :], func=AF.Ln, scale=1.0, bias=onef[0:8, 0:1]), [feB, onefB], [feB])
        P.op("dve", lambda e: e.tensor_scalar(out=fe[:, :], in0=fe[:, :], scalar1=-1.0, scalar2=None, op0=ALU.mult), [feB], [feB])
        P.dma("pool", lambda e, tt=tt: e.dma_start(out=lss[:, tt * 512:(tt + 1) * 512], in_=fe[:, :]), [feB], [io["lssB"]])

    load_wv(2048)
    vt = [C.sb("vt%d" % i, [128, 512], BF16) for i in range(2)]; vtB = [C.buf("vt") for _ in range(2)]
    for c in range(TPC // 128):
        bank, bb = C.bank()
        for k in range(8):
            P.op("pe", lambda e, k=k, bank=bank, c=c: e.matmul(bank[:, :], X[:, k, c * 128:(c + 1) * 128], wv[:, k, :], start=(k == 0), stop=(k == 7)), wvB + [XB], [bb])
        vi = c % 2
        P.op("act", lambda e, bank=bank, vi=vi: e.activation(out=vt[vi][:, :], in_=bank[:, :], func=AF.Copy), [bb], [vtB[vi]])
        P.dma("pool", lambda e, vi=vi, c=c: e.dma_start(out=Vs.rearrange("(h t) d -> t h d", h=8)[c * 128:(c + 1) * 128, :, :], in_=vt[vi][:, :].rearrange("p (h d) -> p h d", h=8)), [vtB[vi]], [io["VsB"]])
    C.release(m0)


def emit_attn(C, io, nqt=SEQ // 512):
    P = C.P
    m0 = C.mark()
    S_ = nqt * 512
    nkt_all = S_ // 128
    qkg = io["qkg"]; Vg = io["Vg"]; lsg = io["lsg"]
    qkgB = io["qkgB"]; VgB = io["VgB"]; lsgB = io["lsgB"]
    osend = io["osend"]; osendB = io["osendB"]
    cidv = lambda: P.vals["sp"]["cid"]
    qk_mine = io["qk_mine"]; V_mine = io["V_mine"]; ls_mine = io["ls_mine"]; mineB = io["mineB"]
    P.dma("sp", lambda e: e.dma_start(out=qk_mine.rearrange("q (r t) -> q r t", r=NCORES), in_=qkg.rearrange("(r q) t -> q r t", r=NCORES)[bass.ds(cidv() * 128, 128), :, :]), [qkgB], [mineB])
    P.dma("sp", lambda e: e.dma_start(out=V_mine.rearrange("(r t) d -> t r d", r=NCORES), in_=Vg.rearrange("(r q) d -> q r d", r=NCORES)[bass.ds(cidv() * TPC, TPC), :, :]), [VgB], [mineB])
    P.dma("sp", lambda e: e.dma_start(out=ls_mine.rearrange("o (r t) -> o r t", r=NCORES), in_=lsg.rearrange("(r h) t -> h r t", r=NCORES)[bass.ds(cidv(), 1), :, :]), [lsgB], [mineB])

    KA = C.sb("KA", [128, S_], BF16)
    VA = C.sb("VA", [128, nkt_all, 128], BF16)
    PW = 512
    NPC = S_ // PW
    kaB = [C.buf("KA") for _ in range(NPC)]
    vaB = [C.buf("VA") for _ in range(NPC)]
    smallB = C.buf("small")
    mask = C.sb("mask", [128, 4, 512], BF16); ident = C.sb("ident", [128, 128], BF16); mcol = C.sb("mcol", [128, 4])
    P.dma("pool", lambda e: e.dma_start(out=mask[:], in_=io["maskneg"]), [], [smallB])
    P.dma("pool", lambda e: e.dma_start(out=ident[:], in_=io["ident"]), [], [smallB])
    P.dma("pool", lambda e: e.dma_start(out=mcol[:], in_=io["mcols"]), [], [smallB])
    zt = C.sb("zt", [64, 2], BF16); ztB = C.buf("zt")
    P.op("pool", lambda e: e.memset(zt[:], 0.0), [], [ztB])
    P.dma("pool", lambda e: e.dma_start(out=osend[:, 0:2], in_=zt[:, :]), [ztB], [osendB])
    lsb = C.sb("lsb", [128, PW]); lsB = C.buf("ls")
    cc = C.sb("cc", [128, PW]); ccB = C.buf("cc")
    c1 = C.sb("c1", [128, PW], BF16); c2 = C.sb("c2", [128, PW], BF16); c3 = C.sb("c3", [128, PW], BF16)
    r1 = C.sb("r1", [128, PW]); r2 = C.sb("r2", [128, PW]); tq = C.sb("tq", [128, PW])
    pcB = C.buf("pc")
    onesr = C.sb("onesr", [128, PW]); onesrB = C.buf("onesr")
    carry = C.sb("carry", [128, 1]); carryB = C.buf("carry")
    P.op("pool", lambda e: e.memset(onesr[:], 1.0), [], [onesrB])
    P.op("pool", lambda e: e.memset(carry[:], 0.0), [], [carryB])
    R = slice(64, 128)
    for pc in range(NPC):
        a, b = pc * PW, (pc + 1) * PW
        r = a // TPC
        la = a % TPC
        k0, k1 = a // 128, b // 128
        P.dma("sp", lambda e, a=a, b=b: e.dma_start(out=KA[0:64, a:b], in_=qk_mine[64:128, a:b]), [mineB], [kaB[pc]])
        P.op("pool", lambda e, k0=k0, k1=k1: e.memset(VA[:, k0:k1, 64:128], 1.0), [], [vaB[pc]])
        P.dma("sp", lambda e, k0=k0, k1=k1: e.dma_start(out=VA[:, k0:k1, 0:64], in_=V_mine.rearrange("(kt p) d -> p kt d", p=128)[:, k0:k1, :]), [mineB], [vaB[pc]])
        P.dma("sp", lambda e, a=a, b=b: e.dma_start(out=lsb[R, :], in_=ls_mine[0:1, a:b].to_broadcast([64, PW])), [mineB], [lsB])
        P.op("dve", lambda e: e.tensor_tensor_scan(out=cc[R, :], data0=onesr[R, :], data1=lsb[R, :], initial=carry[R, 0:1], op0=ALU.mult, op1=ALU.add),
             [onesrB, lsB, carryB], [ccB])
        P.op("dve", lambda e: e.tensor_copy(carry[R, 0:1], cc[R, PW - 1:PW]), [ccB], [carryB])
        P.op("dve", lambda e: e.tensor_copy(c1[R, :], cc[R, :]), [ccB], [pcB])
        P.op("dve", lambda e: e.tensor_tensor(out=r1[R, :], in0=cc[R, :], in1=c1[R, :], op=ALU.subtract), [ccB, pcB], [pcB])
        P.op("dve", lambda e: e.tensor_copy(c2[R, :], r1[R, :]), [pcB], [pcB])
        P.op("dve", lambda e: e.tensor_tensor(out=r2[R, :], in0=r1[R, :], in1=c2[R, :], op=ALU.subtract), [pcB], [pcB])
        P.op("dve", lambda e: e.tensor_copy(c3[R, :], r2[R, :]), [pcB], [pcB])
        P.op("dve", lambda e: e.tensor_scalar(out=tq[R, :], in0=c1[R, :], scalar1=mcol[R, 0:1], scalar2=mcol[R, 1:2], op0=ALU.mult, op1=ALU.add), [pcB, smallB], [pcB])
        P.op("dve", lambda e: e.scalar_tensor_tensor(out=tq[R, :], in0=c2[R, :], scalar=mcol[R, 2:3], in1=tq[R, :], op0=ALU.mult, op1=ALU.add), [pcB, smallB], [pcB])
        P.op("dve", lambda e, a=a, b=b: e.scalar_tensor_tensor(out=KA[R, a:b], in0=c3[R, :], scalar=mcol[R, 3:4], in1=tq[R, :], op0=ALU.mult, op1=ALU.add), [pcB, smallB], [kaB[pc]])

    NPT = 4
    PT = [C.sb("PT", [128, 512], BF16) for i in range(NPT)]
    PTB = [C.buf("PT") for _ in range(NPT)]
    QAt = [C.sb("QAt", [128, 512], BF16) for i in range(2)]
    QAtB = [C.buf("QAt") for _ in range(2)]
    rec = [C.sb("rec", [128, 512]) for i in range(2)]; recB = [C.buf("rec") for _ in range(2)]
    recl = [C.sb("recl", [64, 512]) for i in range(2)]; reclB = [C.buf("recl") for _ in range(2)]
    ot = [C.sb("ot", [64, 512], BF16) for i in range(2)]; otB = [C.buf("ot") for _ in range(2)]
    NSB = 6

    def sbank(i):
        j = i % NSB
        return C.ps[:, j * 512:(j + 1) * 512], C.psb[j]

    def obank(qt):
        j = 6 + (qt % 2)
        return C.ps[:, j * 512:(j + 1) * 512], C.psb[j]

    pairs = [(qt, kt) for qt in range(nqt) for kt in range(4 * qt + 4)]
    LOOK = 2

    def load_q(qt):
        qi = qt % 2
        a = qt * 512
        r = a // TPC
        la = a % TPC
        P.dma("sp", lambda e: e.dma_start(out=QAt[qi][0:64, :], in_=qk_mine[0:64, a:a + 512]), [mineB], [QAtB[qi]])
        P.dma("sp", lambda e: e.dma_start(out=QAt[qi][64:96, :], in_=KA[96:128, a:a + 512]), [kaB[a // PW]], [QAtB[qi]])

    def issue_S(i):
        qt, kt = pairs[i]
        if kt == 0:
            load_q(qt)
        S, SB = sbank(i)
        diag = kt >= 4 * qt
        kp = (kt * 128) // PW
        qi = qt % 2
        P.op("pe", lambda e: e.matmul(S[:, :], KA[0:96, kt * 128:(kt + 1) * 128], QAt[qi][0:96, :], start=True, stop=(not diag)),
             [kaB[kp], QAtB[qi]], [SB])
        if diag:
            P.op("pe", lambda e: e.matmul(S[:, :], ident[:, :], mask[:, kt - 4 * qt, :], start=False, stop=True), [smallB], [SB])
        pi = i % NPT
        P.op("act", lambda e: e.activation(out=PT[pi][:, :], in_=S[:, :], func=AF.Exp), [SB], [PTB[pi]])

    def issue_PV(i):
        qt, kt = pairs[i]
        nkt = 4 * qt + 4
        O, OB = obank(qt)
        pi = i % NPT
        kp = (kt * 128) // PW
        P.op("pe", lambda e: e.matmul(O[:, :], VA[:, kt, :], PT[pi][:, :], start=(kt == 0), stop=(kt == nkt - 1)), [vaB[kp], PTB[pi]], [OB])
        if kt == nkt - 1:
            ri = qt % 2
            P.op("dve", lambda e: e.reciprocal(rec[ri][64:128, :], O[64:128, :]), [OB], [recB[ri]])
            P.dma("pool", lambda e: e.dma_start(out=recl[ri][0:64, :], in_=rec[ri][64:128, :]), [recB[ri]], [reclB[ri]])
            P.op("dve", lambda e: e.tensor_tensor(out=ot[ri][:, :], in0=O[0:64, :], in1=recl[ri][:, :], op=ALU.mult), [OB, reclB[ri]], [otB[ri]])
            P.dma("pool", lambda e: e.dma_start(out=osend[:, 2 + qt * 512:2 + (qt + 1) * 512], in_=ot[ri][:, :]), [otB[ri]], [osendB])

    for i in range(len(pairs) + LOOK):
        if i < len(pairs):
            issue_S(i)
        if i - LOOK >= 0:
            issue_PV(i - LOOK)
    C.release(m0)


def attn_consts():
    kk = np.arange(128)[:, None, None]
    dk = np.arange(4)[None, :, None]
    qq = np.arange(512)[None, None, :]
    maskneg = np.where(dk * 128 + kk <= qq, 0.0, -30000.0).astype(np.float32).astype(NPBF)
    ident = np.eye(128, dtype=np.float32).astype(NPBF)
    mc = np.zeros((128, 4), np.float32)
    mc[67, 0] = -1; mc[96, 0] = 1
    mc[64:67, 1] = 1; mc[99:102, 1] = 1
    mc[68, 2] = -1; mc[97, 2] = 1
    mc[69, 3] = -1; mc[98, 3] = 1
    return dict(maskneg=np.ascontiguousarray(maskneg), ident=ident, mcols=mc)


TWO_PI = 6.283185307179586
MAGIC = 12582912.0


def emit_s5(C, io, nblk=SEQ // 512):
    P = C.P
    m0 = C.mark()
    S_ = nblk * 512
    hng = io["hng"]; hngB = io["hngB"]
    ycs = io["ycs"]; ycsB = io["ycsB"]
    cidv = lambda: P.vals["sp"]["cid"]
    smallB = C.buf("small")
    are = C.sb("are", [128, 4]); aim = C.sb("aim", [128, 4]); ldt = C.sb("ldt", [128, 4])
    bre = C.sb("bre", [128, 4, 16]); bim = C.sb("bim", [128, 4, 16]); cre = C.sb("cre", [128, 4, 16]); cim = C.sb("cim", [128, 4, 16])
    identf = C.sb("identf", [128, 128]); jidx = C.sb("jidx", [128, 128])
    for dst, key in ((are, "are"), (aim, "aim"), (ldt, "ldt"), (bre, "bre"), (bim, "bim"), (cre, "creT"), (cim, "cimT"), (identf, "identf"), (jidx, "jidx")):
        load_f32(C, dst[:], io[key], smallB, "pool")
    zt = C.sb("zt", [128, 2], BF16); ztB = C.buf("zt")
    P.op("pool", lambda e: e.memset(zt[:], 0.0), [], [ztB])
    P.dma("pool", lambda e: e.dma_start(out=ycs[:, 0:2], in_=zt[:, :]), [ztB], [ycsB])
    uT = C.sb("uT", [128, S_], BF16)
    NPC = max(1, S_ // TPC)
    PW = S_ // NPC
    uB = [C.buf("uT") for _ in range(NPC)]
    P.dma("sp", lambda e: e.dma_start(out=uT[:, :].rearrange("p (r t) -> p r t", r=NCORES), in_=hng.rearrange("(r q) t -> q r t", r=NCORES)[bass.ds(cidv() * 128, 128), :, :]), [hngB], uB)

    kB = C.buf("k")
    n_s = [0]

    def st_(shape, dt=F32):
        n_s[0] += 1
        return C.sb("k%d" % n_s[0], shape, dt)

    def dve(fn, extra=()):
        P.op("dve", fn, [kB, smallB] + list(extra), [kB])

    def act(fn):
        P.op("act", fn, [kB, smallB], [kB])

    halfpi = st_([128, 1]); zero = st_([128, 1])
    P.op("pool", lambda e: e.memset(halfpi[:], TWO_PI / 4), [], [kB])
    P.op("pool", lambda e: e.memset(zero[:], 0.0), [kB], [kB])

    def sincos(th, n, cos_out, sin_out):
        t = st_([128, n]); kk = st_([128, n]); tr = st_([128, n]); ab = st_([128, n])
        dve(lambda e: e.tensor_scalar(out=t[:], in0=th, scalar1=1.0 / TWO_PI, scalar2=MAGIC, op0=ALU.mult, op1=ALU.add))
        dve(lambda e: e.tensor_scalar(out=kk[:], in0=t[:], scalar1=-MAGIC, scalar2=None, op0=ALU.add))
        dve(lambda e: e.scalar_tensor_tensor(out=tr[:], in0=kk[:], scalar=-6.28125, in1=th, op0=ALU.mult, op1=ALU.add))
        dve(lambda e: e.scalar_tensor_tensor(out=tr[:], in0=kk[:], scalar=-(TWO_PI - 6.28125), in1=tr[:], op0=ALU.mult, op1=ALU.add))
        dve(lambda e: e.tensor_scalar(out=tr[:], in0=tr[:], scalar1=TWO_PI / 2, scalar2=-TWO_PI / 2, op0=ALU.min, op1=ALU.max))
        act(lambda e: e.activation(out=sin_out, in_=tr[:], func=AF.Sin, bias=zero[:, 0:1]))
        dve(lambda e: e.scalar_tensor_tensor(out=ab[:], in0=tr[:], scalar=-1.0, in1=tr[:], op0=ALU.mult, op1=ALU.max))
        act(lambda e: e.activation(out=cos_out, in_=ab[:], func=AF.Sin, scale=-1.0, bias=halfpi[:, 0:1]))

    dt = st_([128, 4]); mag = st_([128, 4]); th = st_([128, 4]); cth = st_([128, 4]); sth = st_([128, 4])
    act(lambda e: e.activation(out=dt[:], in_=ldt[:], func=AF.Exp))
    dve(lambda e: e.tensor_tensor(out=mag[:], in0=are[:], in1=dt[:], op=ALU.mult))
    act(lambda e: e.activation(out=mag[:], in_=mag[:], func=AF.Exp))
    dve(lambda e: e.tensor_tensor(out=th[:], in0=aim[:], in1=dt[:], op=ALU.mult))
    sincos(th[:], 4, cth[:], sth[:])
    abr = st_([128, 4]); abi = st_([128, 4]); den = st_([128, 4]); cr = st_([128, 4]); ci = st_([128, 4]); t0 = st_([128, 4]); t1 = st_([128, 4])
    dve(lambda e: e.tensor_tensor(out=abr[:], in0=mag[:], in1=cth[:], op=ALU.mult))
    dve(lambda e: e.tensor_tensor(out=abi[:], in0=mag[:], in1=sth[:], op=ALU.mult))
    dve(lambda e: e.tensor_tensor(out=den[:], in0=are[:], in1=are[:], op=ALU.mult))
    dve(lambda e: e.tensor_tensor(out=t0[:], in0=aim[:], in1=aim[:], op=ALU.mult))
    dve(lambda e: e.tensor_tensor(out=den[:], in0=den[:], in1=t0[:], op=ALU.add))
    dve(lambda e: e.reciprocal(den[:], den[:]))
    dve(lambda e: e.tensor_scalar(out=t1[:], in0=abr[:], scalar1=-1.0, scalar2=None, op0=ALU.add))
    dve(lambda e: e.tensor_tensor(out=cr[:], in0=t1[:], in1=are[:], op=ALU.mult))
    dve(lambda e: e.tensor_tensor(out=t0[:], in0=abi[:], in1=aim[:], op=ALU.mult))
    dve(lambda e: e.tensor_tensor(out=cr[:], in0=cr[:], in1=t0[:], op=ALU.add))
    dve(lambda e: e.tensor_tensor(out=cr[:], in0=cr[:], in1=den[:], op=ALU.mult))
    dve(lambda e: e.tensor_tensor(out=ci[:], in0=abi[:], in1=are[:], op=ALU.mult))
    dve(lambda e: e.tensor_tensor(out=t0[:], in0=t1[:], in1=aim[:], op=ALU.mult))
    dve(lambda e: e.tensor_tensor(out=ci[:], in0=ci[:], in1=t0[:], op=ALU.subtract))
    dve(lambda e: e.tensor_tensor(out=ci[:], in0=ci[:], in1=den[:], op=ALU.mult))

    BT = [[C.sb("BT%d_%d" % (ri, st), [128, 128], BF16) for st in range(4)] for ri in range(2)]
    CT = [[C.sb("CT%d_%d" % (ri, st), [128, 128], BF16) for st in range(4)] for ri in range(2)]
    bfull = st_([128, 128]); tb = st_([128, 16])
    for st in range(4):
        for ri in range(2):
            dve(lambda e: e.memset(bfull[:], 0.0))
            for hf in range(2):
                rows = slice(hf * 64, hf * 64 + 64)
                cs = (2 * st + hf) * 16
                if ri == 0:
                    dve(lambda e, rows=rows, st=st: e.tensor_scalar(out=tb[rows, :], in0=bim[rows, st, :], scalar1=ci[rows, st:st + 1], scalar2=None, op0=ALU.mult))
                    dve(lambda e, rows=rows, st=st, cs=cs: e.scalar_tensor_tensor(out=bfull[rows, cs:cs + 16], in0=bre[rows, st, :], scalar=cr[rows, st:st + 1], in1=tb[rows, :], op0=ALU.mult, op1=ALU.subtract))
                else:
                    dve(lambda e, rows=rows, st=st: e.tensor_scalar(out=tb[rows, :], in0=bre[rows, st, :], scalar1=ci[rows, st:st + 1], scalar2=None, op0=ALU.mult))
                    dve(lambda e, rows=rows, st=st, cs=cs: e.scalar_tensor_tensor(out=bfull[rows, cs:cs + 16], in0=bim[rows, st, :], scalar=cr[rows, st:st + 1], in1=tb[rows, :], op0=ALU.mult, op1=ALU.add))
            bank, bb = C.bank()
            P.op("pe", lambda e, bank=bank: e.transpose(bank[:, 0:128], bfull[:, :], identf[:, :]), [kB, smallB], [bb])
            P.op("act", lambda e, bank=bank, ri=ri, st=st: e.activation(out=BT[ri][st][:, :], in_=bank[:, 0:128], func=AF.Copy), [bb], [kB])
            dve(lambda e, ri=ri, st=st: e.memset(CT[ri][st][:, :], 0.0))
            for hf in range(2):
                rows = slice(hf * 64, hf * 64 + 64)
                cs = (2 * st + hf) * 16
                srcc = cre if ri == 0 else cim
                sgn = 1.0 if ri == 0 else -1.0
                dve(lambda e, rows=rows, st=st, cs=cs, srcc=srcc, sgn=sgn, ri=ri: e.tensor_scalar(out=CT[ri][st][rows, cs:cs + 16], in0=srcc[rows, st, :], scalar1=sgn, scalar2=None, op0=ALU.mult))

    phi = st_([128, 512]); cosT = st_([128, 4, 128]); sinT = st_([128, 4, 128]); rt = st_([128, 4, 128])
    onesf = st_([128, 128])
    dve(lambda e: e.memset(onesf[:], 1.0))
    for st in range(4):
        dve(lambda e, st=st: e.tensor_scalar(out=phi[:, st * 128:(st + 1) * 128], in0=jidx[:, :], scalar1=th[:, st:st + 1], scalar2=None, op0=ALU.mult))
        dve(lambda e, st=st: e.tensor_scalar(out=rt[:, st, :], in0=onesf[:, :], scalar1=mag[:, st:st + 1], scalar2=None, op0=ALU.mult))
    sincos(phi[:], 512, cosT[:].rearrange("p a b -> p (a b)"), sinT[:].rearrange("p a b -> p (a b)"))
    thG = st_([128, 4]); Gr = st_([128, 4]); Gi = st_([128, 4])
    dve(lambda e: e.tensor_scalar(out=thG[:], in0=th[:], scalar1=128.0, scalar2=None, op0=ALU.mult))
    sincos(thG[:], 4, Gr[:], Gi[:])

    br = [C.sb("br", [128, 512]) for i in range(2)]; bi = [C.sb("bi", [128, 512]) for i in range(2)]
    brB = [C.buf("br") for _ in range(2)]; biB = [C.buf("bi") for _ in range(2)]
    tA = C.sb("tA", [128, 512]); tBq = C.sb("tBq", [128, 512]); tAB = C.buf("tA"); tBB = C.buf("tB")
    tC = {"dve": C.sb("tC", [128, 512]), "pool": C.sb("tE", [128, 512])}
    tD = {"dve": C.sb("tD", [128, 512]), "pool": C.sb("tF", [128, 512])}
    tCB = {"dve": C.buf("tC"), "pool": C.buf("tE")}; tDB = {"dve": C.buf("tD"), "pool": C.buf("tF")}
    zri = [C.sb("zri", [128, 4, 512]) for _ in range(2)]; zii = [C.sb("zii", [128, 4, 512]) for _ in range(2)]
    zriB = [[C.buf("zri") for _ in range(4)] for _ in range(2)]; ziiB = [[C.buf("zii") for _ in range(4)] for _ in range(2)]
    zr = C.sb("zr", [128, 4, 512]); zi = C.sb("zi", [128, 4, 512])
    zrB = [C.buf("zr") for _ in range(4)]; ziB = [C.buf("zi") for _ in range(4)]
    car_r = C.sb("car_r", [128, 4]); car_i = C.sb("car_i", [128, 4]); carB = C.buf("car")
    ct0 = C.sb("ct0", [128, 4]); ct1 = C.sb("ct1", [128, 4])
    P.op("dve", lambda e: e.memset(car_r[:], 0.0), [], [carB])
    P.op("dve", lambda e: e.memset(car_i[:], 0.0), [carB], [carB])
    xrb = [C.sb("xrb", [128, 512], BF16) for i in range(4)]; xib = [C.sb("xib", [128, 512], BF16) for i in range(4)]
    xrB = [C.buf("xr") for _ in range(4)]; xiB = [C.buf("xi") for _ in range(4)]
    yo = [C.sb("yo", [128, 512], BF16) for i in range(2)]; yoB = [C.buf("yo") for _ in range(2)]

    def v4(ap):
        return ap.rearrange("p (f j) -> p f j", f=4)

    def tb4(tab, st):
        return tab[:, st, :].unsqueeze(1).to_broadcast([128, 4, 128])

    def stage_a(blk):
        c0 = blk * 512
        zb = blk % 2
        ub = uB[c0 // PW]
        for st in range(4):
            i2 = st % 2
            b_r, b_rb = C.bank()
            b_i, b_ib = C.bank()
            P.op("pe", lambda e, st=st, b_r=b_r: e.matmul(b_r[:, :], BT[0][st][:, :], uT[:, c0:c0 + 512], start=True, stop=True), [kB, ub], [b_rb])
            P.op("pe", lambda e, st=st, b_i=b_i: e.matmul(b_i[:, :], BT[1][st][:, :], uT[:, c0:c0 + 512], start=True, stop=True), [kB, ub], [b_ib])
            P.op("act", lambda e, i2=i2, b_r=b_r: e.activation(out=br[i2][:, :], in_=b_r[:, :], func=AF.Copy), [b_rb], [brB[i2]])
            P.op("act", lambda e, i2=i2, b_i=b_i: e.activation(out=bi[i2][:, :], in_=b_i[:, :], func=AF.Copy), [b_ib], [biB[i2]])
            E = "pool"
            P.op(E, lambda e, st=st, i2=i2: e.tensor_tensor(out=v4(tA[:, :]), in0=v4(br[i2][:, :]), in1=tb4(cosT, st), op=ALU.mult), [brB[i2], kB], [tAB])
            P.op(E, lambda e, st=st, i2=i2: e.tensor_tensor(out=v4(tBq[:, :]), in0=v4(bi[i2][:, :]), in1=tb4(sinT, st), op=ALU.mult), [biB[i2], kB], [tBB])
            P.op(E, lambda e, st=st: e.tensor_tensor(out=zri[zb][:, st, :], in0=tA[:, :], in1=tBq[:, :], op=ALU.add), [tAB, tBB], [zriB[zb][st]])
            P.op(E, lambda e, st=st, i2=i2: e.tensor_tensor(out=v4(tA[:, :]), in0=v4(bi[i2][:, :]), in1=tb4(cosT, st), op=ALU.mult), [biB[i2], kB], [tAB])
            P.op(E, lambda e, st=st, i2=i2: e.tensor_tensor(out=v4(tBq[:, :]), in0=v4(br[i2][:, :]), in1=tb4(sinT, st), op=ALU.mult), [brB[i2], kB], [tBB])
            P.op(E, lambda e, st=st: e.tensor_tensor(out=zii[zb][:, st, :], in0=tA[:, :], in1=tBq[:, :], op=ALU.subtract), [tAB, tBB], [ziiB[zb][st]])

    def stage_b(blk):
        zb = blk % 2
        for f in range(4):
            fs = slice(f * 128, (f + 1) * 128)
            for st in range(4):
                P.op("dve", lambda e, st=st, fs=fs: e.tensor_tensor_scan(out=zr[:, st, fs], data0=rt[:, st, :], data1=zri[zb][:, st, fs], initial=car_r[:, st:st + 1], op0=ALU.mult, op1=ALU.add),
                     [kB, zriB[zb][st], carB], [zrB[st]])
                P.op("dve", lambda e, st=st, fs=fs: e.tensor_tensor_scan(out=zi[:, st, fs], data0=rt[:, st, :], data1=zii[zb][:, st, fs], initial=car_i[:, st:st + 1], op0=ALU.mult, op1=ALU.add),
                     [kB, ziiB[zb][st], carB], [ziB[st]])
            lc = f * 128 + 127
            P.op("dve", lambda e, lc=lc: e.tensor_tensor(out=ct0[:, :], in0=zi[:, :, lc], in1=Gi[:, :], op=ALU.mult), ziB + [kB], [carB])
            P.op("dve", lambda e, lc=lc: e.tensor_tensor(out=ct1[:, :], in0=zr[:, :, lc], in1=Gi[:, :], op=ALU.mult), zrB + [kB], [carB])
            P.op("dve", lambda e, lc=lc: e.tensor_tensor(out=car_r[:, :], in0=zr[:, :, lc], in1=Gr[:, :], op=ALU.mult), zrB + [kB], [carB])
            P.op("dve", lambda e: e.tensor_tensor(out=car_r[:, :], in0=car_r[:, :], in1=ct0[:, :], op=ALU.subtract), [carB], [carB])
            P.op("dve", lambda e, lc=lc: e.tensor_tensor(out=car_i[:, :], in0=zi[:, :, lc], in1=Gr[:, :], op=ALU.mult), ziB + [kB], [carB])
            P.op("dve", lambda e: e.tensor_tensor(out=car_i[:, :], in0=car_i[:, :], in1=ct1[:, :], op=ALU.add), [carB], [carB])

    def stage_c(blk):
        c0 = blk * 512
        yb, ybb = C.bank()
        for st in range(4):
            E = "dve" if st == 0 else "pool"
            TC, TD, TCB, TDB = tC[E], tD[E], tCB[E], tDB[E]
            P.op(E, lambda e, st=st, TC=TC: e.tensor_tensor(out=v4(TC[:, :]), in0=v4(zr[:, st, :]), in1=tb4(cosT, st), op=ALU.mult), [zrB[st], kB], [TCB])
            P.op(E, lambda e, st=st, TD=TD: e.tensor_tensor(out=v4(TD[:, :]), in0=v4(zi[:, st, :]), in1=tb4(sinT, st), op=ALU.mult), [ziB[st], kB], [TDB])
            P.op(E, lambda e, st=st, TC=TC, TD=TD: e.tensor_tensor(out=xrb[st][:, :], in0=TC[:, :], in1=TD[:, :], op=ALU.subtract), [TCB, TDB], [xrB[st]])
            P.op(E, lambda e, st=st, TC=TC: e.tensor_tensor(out=v4(TC[:, :]), in0=v4(zr[:, st, :]), in1=tb4(sinT, st), op=ALU.mult), [zrB[st], kB], [TCB])
            P.op(E, lambda e, st=st, TD=TD: e.tensor_tensor(out=v4(TD[:, :]), in0=v4(zi[:, st, :]), in1=tb4(cosT, st), op=ALU.mult), [ziB[st], kB], [TDB])
            P.op(E, lambda e, st=st, TC=TC, TD=TD: e.tensor_tensor(out=xib[st][:, :], in0=TC[:, :], in1=TD[:, :], op=ALU.add), [TCB, TDB], [xiB[st]])
            P.op("pe", lambda e, st=st, yb=yb: e.matmul(yb[:, :], CT[0][st][:, :], xrb[st][:, :], start=(st == 0), stop=False), [kB, xrB[st]], [ybb])
            P.op("pe", lambda e, st=st, yb=yb: e.matmul(yb[:, :], CT[1][st][:, :], xib[st][:, :], start=False, stop=(st == 3)), [kB, xiB[st]], [ybb])
        oi = blk % 2
        P.op("act", lambda e, oi=oi, yb=yb: e.activation(out=yo[oi][:, :], in_=yb[:, :], func=AF.Copy), [ybb], [yoB[oi]])
        P.dma("sp", lambda e, oi=oi: e.dma_start(out=ycs[:, 2 + c0:2 + c0 + 512], in_=yo[oi][:, :]), [yoB[oi]], [ycsB])

    stage_a(0)
    for blk in range(nblk):
        if blk + 1 < nblk:
            stage_a(blk + 1)
        stage_b(blk)
        stage_c(blk)
    C.release(m0)


def s5_consts():
    return dict(identf=np.eye(128, dtype=np.float32), jidx=np.ascontiguousarray(np.broadcast_to(np.arange(128, dtype=np.float32), (128, 128))))


def emit_norm(C, io):
    P = C.P
    m0 = C.mark()
    xs = io["xs"]; xsB = io["xsB"]
    hn_d = io["hn_d"]; hnB_d = io["hn_dB"]
    ones, onesB, eps, epsB = C.K
    NRM = Norm(C, ones, onesB, eps, epsB)
    g = C.sb("g", [128, 8]); gB = C.buf("g")
    load_f32(C, g[:], io["g"], gB, "pool")
    hnt = [C.sb("hnt", [128, 8, 512], BF16) for i in range(2)]; hnB = [C.buf("hn") for _ in range(2)]
    rstd = C.sb("rstd", [128, 512]); rstdB = C.buf("rstd")
    for tt in range(NTT):
        i = tt % 2
        c0 = 2 + tt * 512
        NRM.rstd(lambda k, c0=c0: xs[:, k, c0:c0 + 512], [xsB[tt]], 512, rstd[:, :], rstdB)
        for k in range(8):
            P.op("dve", lambda e, k=k, i=i, c0=c0: e.scalar_tensor_tensor(out=hnt[i][:, k, :], in0=xs[:, k, c0:c0 + 512], scalar=g[:, k:k + 1], in1=rstd[:, :], op0=ALU.mult, op1=ALU.mult),
                 [xsB[tt], gB, rstdB], [hnB[i]])
        P.dma("pool", lambda e, i=i, tt=tt: e.dma_start(out=hn_d.rearrange("(k p) t -> p k t", p=128)[:, :, tt * 512:(tt + 1) * 512], in_=hnt[i][:, :, :]), [hnB[i]], [hnB_d])
    C.release(m0)


I32 = mybir.dt.int32
DEPTH = 4


def allgather(C, src, srcB, dst, dstB):
    C.P.coll(lambda e: e.collective_compute("AllGather", ALU.bypass, replica_groups=[list(range(NCORES))], ins=[src], outs=[dst]), [srcB], [dstB])


def build_fused():
    C = Ctx()
    P = C.P
    TH = TPC + 2
    xin = C.din("xin", [128, 8, TPC]); pT = C.din("pT", [DEPTH, 128, 2, TPC]); hmask = C.din("hmask", [128, 1])
    cid = C.din("cid", [1, 1], I32); pid = C.din("pid", [1, 1], I32)
    gmix = C.din("gmixc", [DEPTH, 128, 8]); gffn = C.din("gffnc", [DEPTH, 128, 8])
    win = C.din("win", [2, D, 2568]); vgain = C.din("vgain", [2, 128, 512]); wsT = C.din("wsT", [2, 128, 4, 128]); tri = C.din("tri", [128, 128])
    bsb = C.din("bsb", [2, 128, 512]); qg = C.din("qg", [2, 64, 1]); kg = C.din("kg", [2, 64, 1]); bfg = C.din("bfg", [2, 8, 1])
    wout = C.din("wout", [2, D, D])
    maskneg = C.din("maskneg", [128, 4, 512], BF16); ident = C.din("ident", [128, 128], BF16); mcols = C.din("mcols", [128, 4])
    are = C.din("are", [2, 128, 4]); aim = C.din("aim", [2, 128, 4]); ldt = C.din("ldt", [2, 128, 4])
    bre = C.din("bre", [2, 128, 4, 16]); bim = C.din("bim", [2, 128, 4, 16]); creT = C.din("creT", [2, 128, 4, 16]); cimT = C.din("cimT", [2, 128, 4, 16])
    identf = C.din("identf", [128, 128]); jidx = C.din("jidx", [128, 128])
    dvec = C.din("dvec", [2, 128, 8]); wglu = C.din("wglu", [2, D, 2 * D])
    wup = C.din("wup", [DEPTH, D, 2 * DFF]); cw = C.din("cw", [DEPTH, 128, 3, 2 * NJ]); cb = C.din("cb", [DEPTH, 128, 2 * NJ])
    wdn = C.din("wdn", [DEPTH, DFF, D]); wgate = C.din("wgate", [DEPTH, D, D]); wproj = C.din("wproj", [DEPTH, 256, D])
    xout = C.dout("xout", [128, 8, TPC])
    hn_d = C.dint("hn_d", [D, TPC], BF16); hng = C.dint("hng", [NCORES * D, TPC], BF16)
    ya_d = C.dint("ya_d", [128, 4, TPC], BF16)
    qks = C.dint("qks", [8 * 128, TPC], BF16); qkg = C.dint("qkg", [NCORES * 8 * 128, TPC], BF16)
    Vs = C.dint("Vs", [8 * TPC, 64], BF16); Vg = C.dint("Vg", [NCORES * 8 * TPC, 64], BF16)
    lss = C.dint("lss", [8, TPC]); lsg = C.dint("lsg", [NCORES * 8, TPC])
    hsend = C.dint("hsend", [128, 24]); hrecv = C.dint("hrecv", [NCORES * 128, 24])
    osend = C.dint("osend", [64, SEQ + 2], BF16); og = C.dint("og", [NCORES * 64, SEQ + 2], BF16)
    ycs = C.dint("ycs", [128, SEQ + 2], BF16); ycg = C.dint("ycg", [NCORES * 128, SEQ + 2], BF16)
    B = {k: C.buf(k) for k in ("hn_d", "hng", "ya_d", "qks", "qkg", "Vs", "Vg", "lss", "lsg", "hsend", "hrecv", "osend", "og", "ycs", "ycg", "out")}

    P.eng_init = {"sp": lambda h: {"cid": h.value_load(cid)}, "act": lambda h: {"cid": h.value_load(cid), "pid": h.value_load(pid)}}
    qk_mine = C.dint("qk_mine", [128, SEQ], BF16); V_mine = C.dint("V_mine", [SEQ, 64], BF16); ls_mine = C.dint("ls_mine", [1, SEQ])
    B["mine"] = C.buf("mine")
    C.K = consts(C)
    xs = C.sb("xs", [128, 8, TH], F32)
    xsB = [C.buf("xs") for _ in range(NTT)]
    xsH = C.buf("xsH")
    for k in range(8):
        for tt in range(NTT):
            c0 = 2 + tt * 512
            P.dma("sp", lambda e, k=k, c0=c0, tt=tt: e.dma_start(out=xs[:, k, c0:c0 + 512], in_=xin[:, k, tt * 512:(tt + 1) * 512]), [], [xsB[tt]])
    base = dict(xs=xs, xsB=xsB, xsH=xsH, outB=B["out"], hn_d=hn_d, hn_dB=B["hn_d"])
    emit_norm(C, dict(base, g=gmix[0]))
    for l in range(DEPTH):
        if l % 2 == 0:
            e_ = l // 2
            emit_pre_even(C, dict(base, win=win[e_], vgain=vgain[e_], wsT=wsT[e_], tri=tri, bsb=bsb[e_], qg=qg[e_], kg=kg[e_], bfg=bfg[e_],
                                  ya_d=ya_d, ya_dB=B["ya_d"], qks=qks, qksB=B["qks"], Vs=Vs, VsB=B["Vs"], lss=lss, lssB=B["lss"], hsend=hsend, hsendB=B["hsend"]))
            allgather(C, qks, B["qks"], qkg, B["qkg"])
            allgather(C, Vs, B["Vs"], Vg, B["Vg"])
            allgather(C, lss, B["lss"], lsg, B["lsg"])
            allgather(C, hsend, B["hsend"], hrecv, B["hrecv"])
            emit_attn(C, dict(qk_mine=qk_mine, V_mine=V_mine, ls_mine=ls_mine, mineB=B["mine"], qkg=qkg, qkgB=B["qkg"], Vg=Vg, VgB=B["Vg"], lsg=lsg, lsgB=B["lsg"], osend=osend, osendB=B["osend"],
                              maskneg=maskneg, ident=ident, mcols=mcols))
            allgather(C, osend, B["osend"], og, B["og"])
            mode = "even"
            extra = dict(wmix=wout[e_], ya_d=ya_d, ya_dB=B["ya_d"], og=og, ogB=B["og"])
        else:
            o_ = l // 2
            m0 = C.mark()
            hs = C.sb("hs", [128, 24]); hsB = C.buf("hs")
            P.op("dve", lambda e, hs=hs: e.memset(hs[:, :], 0.0), [], [hsB])
            P.op("dve", lambda e, hs=hs: e.tensor_copy(hs[:, 0:16].rearrange("p (k t) -> p k t", k=8), xs[:, :, TPC:TPC + 2]), [xsB[NTT - 1], hsB], [hsB])
            P.dma("pool", lambda e, hs=hs: e.dma_start(out=hsend, in_=hs[:, :]), [hsB], [B["hsend"]])
            C.release(m0)
            allgather(C, hsend, B["hsend"], hrecv, B["hrecv"])
            allgather(C, hn_d, B["hn_d"], hng, B["hng"])
            emit_s5(C, dict(hng=hng, hngB=B["hng"], ycs=ycs, ycsB=B["ycs"], are=are[o_], aim=aim[o_], ldt=ldt[o_], bre=bre[o_], bim=bim[o_],
                            creT=creT[o_], cimT=cimT[o_], identf=identf, jidx=jidx))
            allgather(C, ycs, B["ycs"], ycg, B["ycg"])
            mode = "odd"
            extra = dict(wmix=wglu[o_], gmix=gmix[l], dvec=dvec[o_], ycg=ycg, ycgB=B["ycg"])
        last = (l == DEPTH - 1)
        io = dict(base, gffn=gffn[l], wup=wup[l], cw=cw[l], cb=cb[l], wdn=wdn[l], wgate=wgate[l], wproj=wproj[l], pT=pT[l],
                  gnext=gmix[(l + 1) % DEPTH], hmask=hmask, hrecv=hrecv, hrecvB=B["hrecv"], **extra)
        if last:
            io["hn_d"] = None
            io["xout"] = xout
        emit_post(C, mode, io)
    return C.finish()


def _col(v):
    return np.ascontiguousarray(np.asarray(v, np.float32).reshape(-1, 128).T)


def _f3(a):
    F, T = a.shape
    return np.ascontiguousarray(a.reshape(F // 128, 128, T).transpose(1, 0, 2))


def _s5lay(a):
    a = np.asarray(a, np.float32)
    a = a.reshape(4, 2, 64, *a.shape[2:])
    a = np.moveaxis(a, 0, 2)
    return np.ascontiguousarray(a.reshape(128, 4, *a.shape[3:]))


def kernel(x, p, norm_mix, norm_ffn, ev_w_in, ev_b_fgate, ev_q_norm, ev_k_norm, ev_v_norm,
           ev_w_spatial, ev_b_spatial, ev_w_out, od_a_re, od_a_im, od_log_dt, od_b_re, od_b_im,
           od_c_re, od_c_im, od_d, od_w_glu, ffn_w_up, ffn_conv_w, ffn_conv_b, ffn_w_down,
           ple_w_proj, ple_w_gate):
    A = lambda a: np.ascontiguousarray(np.asarray(a, np.float32))
    x = A(x); p = A(p)
    XT = np.ascontiguousarray(x[0].T)
    ac = attn_consts()
    sc = s5_consts()
    com = dict(
        gmixc=np.stack([_col(norm_mix[l]) for l in range(DEPTH)]), gffnc=np.stack([_col(norm_ffn[l]) for l in range(DEPTH)]),
        win=A(ev_w_in), vgain=np.stack([np.broadcast_to(A(ev_v_norm[e]), (128, 512)) for e in range(2)]).copy(),
        wsT=np.stack([A(ev_w_spatial[e]).transpose(2, 0, 1) for e in range(2)]).copy(), tri=np.triu(np.ones((128, 128), np.float32)),
        bsb=np.stack([np.broadcast_to(A(ev_b_spatial[e]).reshape(-1), (128, 512)) for e in range(2)]).copy(),
        qg=A(ev_q_norm).reshape(2, 64, 1), kg=A(ev_k_norm).reshape(2, 64, 1), bfg=A(ev_b_fgate).reshape(2, 8, 1), wout=A(ev_w_out),
        dvec=np.stack([_col(od_d[o]) for o in range(2)]), wglu=A(od_w_glu),
        wup=A(ffn_w_up), cw=np.stack([A(ffn_conv_w[l]).reshape(3, 2 * NJ, 128).transpose(2, 0, 1) for l in range(DEPTH)]).copy(),
        cb=np.stack([_col(ffn_conv_b[l]) for l in range(DEPTH)]), wdn=A(ffn_w_down), wgate=A(ple_w_gate), wproj=A(ple_w_proj),
        **ac, **sc)
    maps = []
    for c in range(NCORES):
        gs = slice(8 * c, 8 * c + 8)
        m = dict(com)
        m.update(
            xin=_f3(XT[:, c * TPC:(c + 1) * TPC]),
            pT=np.stack([p[l, 0, c * TPC:(c + 1) * TPC, :].T.reshape(2, 128, TPC).transpose(1, 0, 2) for l in range(DEPTH)]).copy(),
            hmask=np.full((128, 1), 0.0 if c == 0 else 1.0, np.float32),
            cid=np.array([[c]], np.int32), pid=np.array([[(c - 1) % NCORES]], np.int32),
            are=np.stack([_s5lay(od_a_re[o][gs]) for o in range(2)]), aim=np.stack([_s5lay(od_a_im[o][gs]) for o in range(2)]),
            ldt=np.stack([_s5lay(np.broadcast_to(A(od_log_dt[o][gs])[:, None], (8, 64))) for o in range(2)]),
            bre=np.stack([_s5lay(od_b_re[o][gs]) for o in range(2)]), bim=np.stack([_s5lay(od_b_im[o][gs]) for o in range(2)]),
            creT=np.stack([_s5lay(A(od_c_re[o][gs]).transpose(0, 2, 1)) for o in range(2)]),
            cimT=np.stack([_s5lay(A(od_c_im[o][gs]).transpose(0, 2, 1)) for o in range(2)]))
        maps.append(m)
    nc = build_fused()
    res = run_bass_kernel_spmd(nc, maps, core_ids=list(range(NCORES)))
    XT = np.concatenate([res.results[c]["xout"].transpose(1, 0, 2).reshape(D, TPC) for c in range(NCORES)], axis=1)
    return np.ascontiguousarray(XT.T)[None].astype(np.float32)
```

```python
import numpy as np
import concourse.bass as bass
import concourse.mybir as mybir
from concourse.bass_utils import run_bass_kernel_spmd
from contextlib import ExitStack

F32 = mybir.dt.float32
BF16 = mybir.dt.bfloat16
ALU = mybir.AluOpType
AF = mybir.ActivationFunctionType
AX = mybir.AxisListType


class Buf:
    __slots__ = ("name", "w", "r")

    def __init__(self, name=""):
        self.name = name
        self.w = None
        self.r = []


class Op:
    __slots__ = ("eng", "fn", "deps", "signal", "sem", "val", "is_dma", "idx", "inc")

    def __init__(self, eng, fn, is_dma=False):
        self.eng = eng
        self.fn = fn
        self.deps = []
        self.signal = False
        self.sem = None
        self.val = 0
        self.is_dma = is_dma
        self.idx = 0
        self.inc = 16


ENGS = ("pe", "act", "dve", "pool", "sp")
N_DMA_SLOTS = 8


class Prog:
    def __init__(self, nc, strict_same_engine=True):
        self.nc = nc
        self.ops = {e: [] for e in ENGS}
        self.last = {e: None for e in ENGS}
        self.strict = strict_same_engine
        self.dma_slot_last = {}
        self.dma_slot_cnt = {}
        self.dma_rr = {e: 0 for e in ENGS}
        self.pending_bar = {e: [] for e in ENGS}
        self.n_ops = 0
        self.vals = {}
        self.eng_init = {}

    def _add(self, op, reads, writes):
        deps = []
        for b in reads:
            if b.w is not None:
                deps.append(b.w)
        for b in writes:
            if b.w is not None:
                deps.append(b.w)
            deps.extend(b.r)
        if self.pending_bar[op.eng]:
            deps.extend(self.pending_bar[op.eng])
            self.pending_bar[op.eng] = []
        seen = set()
        for d in deps:
            if d is op or id(d) in seen:
                continue
            seen.add(id(d))
            if (not d.is_dma) and d.eng == op.eng and (op.eng == "pe" or not self.strict):
                continue
            op.deps.append(d)
            d.signal = True
        for b in reads:
            if not op.is_dma:
                b.r = [x for x in b.r if x.is_dma or x.eng != op.eng]
            b.r.append(op)
        for b in writes:
            b.w = op
            b.r = []
        op.idx = len(self.ops[op.eng])
        self.ops[op.eng].append(op)
        self.last[op.eng] = op
        self.n_ops += 1
        return op

    def op(self, eng, fn, reads=(), writes=()):
        return self._add(Op(eng, fn), list(reads), list(writes))

    def dma(self, eng, fn, reads=(), writes=()):
        op = Op(eng, fn, is_dma=True)
        slot = self.dma_rr[eng]
        self.dma_rr[eng] = (slot + 1) % N_DMA_SLOTS
        key = (eng, slot)
        prev = self.dma_slot_last.get(key)
        op.sem = key
        self.dma_slot_cnt[key] = self.dma_slot_cnt.get(key, 0) + 16
        op.val = self.dma_slot_cnt[key]
        self._add(op, list(reads), list(writes))
        if prev is not None and prev not in op.deps:
            op.deps.append(prev)
        self.dma_slot_last[key] = op
        return op

    def coll(self, fn, reads=(), writes=()):
        op = Op("pool", fn, is_dma=True)
        op.inc = 1
        key = ("cc", 0)
        prev = self.dma_slot_last.get(key)
        op.sem = key
        self.dma_slot_cnt[key] = self.dma_slot_cnt.get(key, 0) + 1
        op.val = self.dma_slot_cnt[key]
        self._add(op, list(reads), list(writes))
        if prev is not None and prev not in op.deps:
            op.deps.append(prev)
        self.dma_slot_last[key] = op
        return op

    def barrier(self):
        lasts = [o for o in self.last.values() if o is not None]
        lasts += list(self.dma_slot_last.values())
        for e in ENGS:
            self.pending_bar[e] = list(lasts)

    def emit(self):
        nc = self.nc
        self.barrier()
        fin = Op("sp", None)
        self._add(fin, [], [])
        for e in ENGS:
            c = 0
            for o in self.ops[e]:
                if o.is_dma:
                    continue
                if o.signal:
                    c += 1
                    o.val = c
                    o.sem = ("eng", e)
            assert c < 1000000, (e, c)
        for k, v in self.dma_slot_cnt.items():
            assert v < 1000000, (k, v)
        with ExitStack() as es:
            sems = {}
            for e in ENGS:
                sems[("eng", e)] = es.enter_context(nc.semaphore("s_" + e))
            for key in self.dma_slot_cnt:
                sems[key] = es.enter_context(nc.semaphore("d_%s%d" % key))
            block = es.enter_context(nc.Block())

            def run(engname, handle):
                seen = {}
                init = getattr(self, "eng_init", {}).get(engname)
                if init is not None:
                    self.vals[engname] = init(handle)
                for o in self.ops[engname]:
                    need = {}
                    for d in o.deps:
                        if need.get(d.sem, 0) < d.val:
                            need[d.sem] = d.val
                    for s, v in need.items():
                        if seen.get(s, 0) >= v:
                            continue
                        handle.wait_ge(sems[s], v)
                        seen[s] = v
                    if o.fn is None:
                        continue
                    ins = o.fn(handle)
                    if o.is_dma:
                        ins.then_inc(sems[o.sem], o.inc)
                    elif o.signal:
                        ins.then_inc(sems[o.sem], 1)

            @block.tensor
            def _(h):
                run("pe", h)

            @block.scalar
            def _(h):
                run("act", h)

            @block.vector
            def _(h):
                run("dve", h)

            @block.gpsimd
            def _(h):
                run("pool", h)

            @block.sync
            def _(h):
                run("sp", h)


NCORES = 8
D = 1024
SEQ = 16384
TPC = SEQ // NCORES
NTT = TPC // 512
DFF = 2816
NJ = DFF // 128
EPS = 1e-6
import ml_dtypes
NPBF = ml_dtypes.bfloat16


import os
USE_WCACHE = os.environ.get('K_WCACHE', '1') == '1'
USE_POOLCONV = os.environ.get('K_POOLCONV', '0') == '1'
ARENA_BYTES = 212480


class Ctx:
    def __init__(self):
        self.nc = bass.Bass("TRN2", target_bir_lowering=False)
        self.es = ExitStack()
        self.P = Prog(self.nc)
        self.ps = self.es.enter_context(self.nc.psum_tensor("ps", [128, 8 * 512], F32))
        self.psb = [Buf("ps%d" % i) for i in range(8)]
        self.psi = 0
        self.nb = 0
        self.arena = self.es.enter_context(self.nc.sbuf_tensor("arena", [128, ARENA_BYTES], mybir.dt.uint8))
        self.off = 0
        self.peak = 0

    def sb(self, name, shape, dt=F32):
        n = 1
        for d_ in shape[1:]:
            n *= d_
        nbytes = n * mybir.dt.size(dt)
        off = (self.off + 63) // 64 * 64
        assert off + nbytes <= ARENA_BYTES, ("SBUF arena overflow", name, off, nbytes)
        self.off = off + nbytes
        self.peak = max(self.peak, self.off)
        ap = self.arena[0:shape[0], off:off + nbytes].bitcast(dt)
        if len(shape) == 3:
            ap = ap.rearrange("p (a b) -> p a b", a=shape[1])
        return ap

    def mark(self):
        return self.off

    def release(self, m):
        self.P.barrier()
        self.off = m

    def din(self, name, shape, dt=F32):
        return self.nc.dram_tensor(name, list(shape), dt, kind="ExternalInput").ap()

    def dout(self, name, shape, dt=F32):
        return self.nc.dram_tensor(name, list(shape), dt, kind="ExternalOutput").ap()

    def dint(self, name, shape, dt=F32):
        return self.nc.dram_tensor(name, list(shape), dt).ap()

    def wcache_get(self, key):
        if not hasattr(self, "wkeys"):
            self.wkeys = {}
            self.wcache = self.dint("wcache", [160, 128, 1408], BF16)
        if key not in self.wkeys:
            idx = len(self.wkeys)
            assert idx < 160
            self.wkeys[key] = (self.wcache[idx], self.buf("wc"))
        return self.wkeys[key]

    def bank(self):
        i = self.psi
        self.psi = (i + 1) % 8
        return self.ps[:, i * 512:(i + 1) * 512], self.psb[i]

    def buf(self, name=""):
        self.nb += 1
        return Buf(name + str(self.nb))

    def finish(self):
        self.P.emit()
        self.es.close()
        return self.nc


class WLoader:
    def __init__(self, C, maxelems, nbuf=2, cast_eng="pool", dma_eng="sp", alt_act=False):
        self.C = C
        self.ncast = 0
        self.alt_act = alt_act
        self.st = [C.sb("wst", [128, maxelems], F32) for i in range(nbuf)]
        self.sbuf = [C.buf("wst") for _ in range(nbuf)]
        self.i = 0
        self.n = nbuf
        self.maxelems = maxelems
        self.cast_eng = cast_eng
        self.dma_eng = dma_eng

    def load(self, src, dst, dstbuf, K, M, key=None, first=True):
        C = self.C
        P = C.P
        assert K * M <= self.maxelems
        if not USE_WCACHE:
            key = None
        if key is not None:
            cap, cB = C.wcache_get(key)
            cview = cap[:, 0:K * M].rearrange("p (k m) -> p k m", k=K)
            if not first:
                P.dma(self.dma_eng, lambda e: e.dma_start(out=dst, in_=cview), [cB], [dstbuf])
                return
        i = self.i
        self.i = (i + 1) % self.n
        st = self.st[i][:, 0:K * M].rearrange("p (k m) -> p k m", k=K)
        P.dma(self.dma_eng, lambda e: e.dma_start(out=st, in_=src), [], [self.sbuf[i]])
        self.ncast += 1
        if self.alt_act and self.ncast % 2 == 0:
            P.op("act", lambda e: e.activation(out=dst, in_=st, func=AF.Copy), [self.sbuf[i]], [dstbuf])
        else:
            P.op(self.cast_eng, lambda e: e.tensor_copy(dst, st), [self.sbuf[i]], [dstbuf])
        if key is not None:
            P.dma("act", lambda e: e.dma_start(out=cview, in_=dst), [dstbuf], [cB])


def load_f32(C, dst, src, buf, eng="sp"):
    C.P.dma(eng, lambda e: e.dma_start(out=dst, in_=src), [], [buf])


class Norm:
    def __init__(self, C, ones_bf, onesB, epscol, epsB):
        self.C = C
        self.sq = [C.sb("nsq%d" % i, [128, 512], BF16) for i in range(2)]
        self.sqB = [C.buf("nsq") for _ in range(2)]
        self.i = 0
        self.ones = ones_bf
        self.onesB = onesB
        self.eps = epscol
        self.epsB = epsB

    def rstd(self, xk, xbufs, n, out, outB, nk=8, mean_div=1024.0, ones=None):
        C, P = self.C, self.C.P
        bank, bb = C.bank()
        ones = self.ones if ones is None else ones
        np_ = out.shape[0]
        for k in range(nk):
            i = self.i
            self.i ^= 1
            sq = self.sq[i][0:np_, 0:n]
            src = xk(k)
            P.op("act", lambda e, sq=sq, src=src: e.activation(out=sq, in_=src, func=AF.Square), xbufs, [self.sqB[i]])
            P.op("pe", lambda e, sq=sq, k=k: e.matmul(bank[0:np_, 0:n], ones[0:np_, 0:np_], sq, start=(k == 0), stop=(k == nk - 1)),
                 [self.sqB[i], self.onesB], [bb])
        P.op("act", lambda e: e.activation(out=out, in_=bank[0:np_, 0:n], func=AF.Ln, scale=1.0 / mean_div, bias=self.eps[0:np_, 0:1]),
             [bb, self.epsB], [outB])
        P.op("act", lambda e: e.activation(out=out, in_=out, func=AF.Exp, scale=-0.5), [outB], [outB])


def consts(C):
    P = C.P
    ones = C.sb("ones_bf", [128, 128], BF16)
    onesB = C.buf("ones")
    P.op("pool", lambda e: e.memset(ones[:], 1.0), [], [onesB])
    eps = C.sb("epscol", [128, 1], F32)
    epsB = C.buf("eps")
    P.op("pool", lambda e: e.memset(eps[:], EPS), [], [epsB])
    return ones, onesB, eps, epsB


def emit_post(C, mode, io):
    P = C.P
    m0 = C.mark()
    dbg = 3
    TH = TPC + 2
    wmix = io["wmix"]
    gffn_d = io["gffn"]; wup = io["wup"]; cw_d = io["cw"]; cb_d = io["cb"]; wdn = io["wdn"]
    wgate = io["wgate"]; wproj = io["wproj"]; pT_d = io["pT"]; gnext_d = io["gnext"]; hmask_d = io["hmask"]
    xout = io.get("xout"); hn_d = io.get("hn_d")
    want_hn = hn_d is not None
    outB = io["outB"]; hnB_d = io["hn_dB"]

    ones, onesB, eps, epsB = C.K
    NRM = Norm(C, ones, onesB, eps, epsB)
    WL = WLoader(C, 11 * 128, nbuf=2, alt_act=True)

    xs = io["xs"]; xsB = io["xsB"]; xsH = io["xsH"]
    ym = C.sb("ym", [128, 8, TH], BF16)
    ymB = C.buf("ym")
    smallB = C.buf("small")
    gffn = C.sb("gffn", [128, 8]); cw = C.sb("cw", [128, 3, 2 * NJ]); cb = C.sb("cb", [128, 2 * NJ])
    gnext = C.sb("gnext", [128, 8]); hmask = C.sb("hmask", [128, 1])
    for dst, src in ((gffn, gffn_d), (cw, cw_d), (cb, cb_d), (gnext, gnext_d), (hmask, hmask_d)):
        load_f32(C, dst[:], src, smallB, "pool")
    if mode == "odd":
        gmix = C.sb("gmix", [128, 8]); dvec = C.sb("dvec", [128, 8]); gd = C.sb("gd", [128, 8])
        gdB = C.buf("gd")
        load_f32(C, gmix[:], io["gmix"], smallB, "pool")
        load_f32(C, dvec[:], io["dvec"], smallB, "pool")
        P.op("dve", lambda e: e.tensor_tensor(out=gd[:], in0=gmix[:], in1=dvec[:], op=ALU.mult), [smallB], [gdB])
    hal = C.sb("hal", [128, 24]); halB = C.buf("hal")
    hrecv = io["hrecv"]; hrecvB = io["hrecvB"]
    P.dma("act", lambda e: e.dma_start(out=hal[:, :], in_=hrecv[bass.ds(P.vals["act"]["pid"] * 128, 128), :]), [hrecvB], [halB])
    P.op("dve", lambda e: e.tensor_copy(xs[:, :, 0:2], hal[:, 0:16].rearrange("p (k t) -> p k t", k=8)), [halB], [xsH])
    cidv = lambda: P.vals["act"]["cid"]
    if mode == "even":
        ya_d = io["ya_d"]; og = io["og"]
        P.dma("sp", lambda e: e.dma_start(out=ym[:, 0:4, 2:TH], in_=ya_d), [io["ya_dB"]], [ymB])
        P.op("dve", lambda e: e.tensor_copy(ym[:, 0:4, 0:2], hal[:, 16:24].rearrange("p (k t) -> p k t", k=4)), [halB], [ymB])
        P.dma("act", lambda e: e.dma_start(out=ym[:, 4:8, :], in_=og.rearrange("(m p) t -> p m t", p=128)[:, :, bass.ds(cidv() * TPC, TH)]), [io["ogB"]], [ymB])
    else:
        ycg = io["ycg"]
        P.dma("act", lambda e: e.dma_start(out=ym[:, :, :], in_=ycg.rearrange("(k p) t -> p k t", p=128)[:, :, bass.ds(cidv() * TPC, TH)]), [io["ycgB"]], [ymB])

    def xbuf(c0):
        return xsH if c0 == 0 else xsB[(c0 - 2) // 512]

    NWT = 6
    wt = [C.sb("wt%d" % i, [128, 8, 128], BF16) for i in range(NWT)]
    wtB = [C.buf("wt") for _ in range(NWT)]
    wti = [0]

    def next_wt():
        i = wti[0]
        wti[0] = (i + 1) % NWT
        return wt[i], wtB[i]

    wd = [C.sb("wd%d" % i, [128, NJ, 128], BF16) for i in range(2)]
    wdB = [[C.buf("wd") for _ in range(2)] for _ in range(2)]
    rstd = C.sb("rstd", [128, 512], F32); rstdB = C.buf("rstd")
    xh = [C.sb("xh%d" % i, [128, 8, 514], BF16) for i in range(2)]
    xhB = [C.buf("xh") for _ in range(2)]
    xhH = [C.buf("xhH") for _ in range(2)]
    aT = C.sb("aT", [128, NJ, 512], BF16); aTB = [C.buf("aT") for _ in range(NJ)]
    hsb = [C.sb("hsb%d" % i, [128, 514], F32) for i in range(2)]; hsbB = [C.buf("hsb") for _ in range(2)]
    acc = [C.sb("acc%d" % i, [128, 512], F32) for i in range(2)]; accB = [C.buf("acc") for _ in range(2)]
    sg = C.sb("sg", [128, 512], F32); sgB = C.buf("sg")
    hsave = C.sb("hsave", [128, 2 * NJ, 2], F32); hhB = [C.buf("hh") for _ in range(2 * NJ)]
    pT = C.sb("pT", [128, 2, 512], BF16); pTB = C.buf("pT")
    wpj = [C.sb("wpj%d" % i, [128, 2, 128], BF16) for i in range(2)]; wpjB = [C.buf("wpj") for _ in range(2)]
    if mode == "odd":
        ygt = C.sb("ygt", [128, 8, 514], BF16); ygB = C.buf("yg")
        tmpf = C.sb("tmpf", [128, 512], F32); tmpB = C.buf("tmpf")
    hnt = C.sb("hnt", [128, 8, 512], BF16); hnB = C.buf("hnt")

    def wsrc(w, c0, ncol=128, K=8):
        return w.rearrange("(k p) c -> p k c", p=128)[:, :, c0:c0 + ncol]

    def mixer(tt):
        cols = [(2 + tt * 512, 512)]
        if tt == 0:
            cols = [(0, 2)] + cols
        if mode == "even":
            for fo in range(8):
                w, wB = next_wt()
                WL.load(wsrc(wmix, fo * 128), w[:], wB, 8, 128, key=("mix", fo), first=(tt == 0))
                for (c0, n) in cols:
                    bank, bb = C.bank()
                    for k in range(8):
                        P.op("pe", lambda e, k=k, c0=c0, n=n, bank=bank, w=w: e.matmul(bank[:, 0:n], w[:, k, :], ym[:, k, c0:c0 + n], start=(k == 0), stop=(k == 7)),
                             [wB, ymB], [bb])
                    xb = xbuf(c0)
                    P.op("dve", lambda e, fo=fo, c0=c0, n=n, bank=bank: e.tensor_tensor(out=xs[:, fo, c0:c0 + n], in0=xs[:, fo, c0:c0 + n], in1=bank[:, 0:n], op=ALU.add),
                         [bb, xb], [xb])
        else:
            for (c0, n) in cols:
                yc0 = 0 if c0 == 0 else 2
                xb = xbuf(c0)
                NRM.rstd(lambda k, c0=c0, n=n: xs[:, k, c0:c0 + n], [xb], n, rstd[:, 0:n], rstdB)
                for k in range(8):
                    P.op("dve", lambda e, k=k, c0=c0, n=n: e.scalar_tensor_tensor(out=tmpf[:, 0:n], in0=xs[:, k, c0:c0 + n], scalar=gd[:, k:k + 1], in1=rstd[:, 0:n], op0=ALU.mult, op1=ALU.mult),
                         [xb, gdB, rstdB], [tmpB])
                    P.op("dve", lambda e, k=k, c0=c0, n=n: e.tensor_tensor(out=tmpf[:, 0:n], in0=tmpf[:, 0:n], in1=ym[:, k, c0:c0 + n], op=ALU.add),
                         [tmpB, ymB], [tmpB])
                    P.op("act", lambda e, k=k, n=n, yc0=yc0: e.activation(out=ygt[:, k, yc0:yc0 + n], in_=tmpf[:, 0:n], func=AF.Gelu), [tmpB], [ygB])
            for fo in range(8):
                wa, waB = next_wt()
                WL.load(wsrc(wmix, fo * 128), wa[:], waB, 8, 128, key=("mixa", fo), first=(tt == 0))
                wb, wbB = next_wt()
                WL.load(wsrc(wmix, D + fo * 128), wb[:], wbB, 8, 128, key=("mixb", fo), first=(tt == 0))
                for (c0, n) in cols:
                    yc0 = 0 if c0 == 0 else 2
                    ba, bab = C.bank()
                    bg, bgb = C.bank()
                    for k in range(8):
                        P.op("pe", lambda e, k=k, n=n, yc0=yc0, ba=ba, wa=wa: e.matmul(ba[:, 0:n], wa[:, k, :], ygt[:, k, yc0:yc0 + n], start=(k == 0), stop=(k == 7)), [waB, ygB], [bab])
                    for k in range(8):
                        P.op("pe", lambda e, k=k, n=n, yc0=yc0, bg=bg, wb=wb: e.matmul(bg[:, 0:n], wb[:, k, :], ygt[:, k, yc0:yc0 + n], start=(k == 0), stop=(k == 7)), [wbB, ygB], [bgb])
                    P.op("act", lambda e, n=n, bg=bg: e.activation(out=sg[:, 0:n], in_=bg[:, 0:n], func=AF.Sigmoid), [bgb], [sgB])
                    P.op("dve", lambda e, n=n, ba=ba: e.tensor_tensor(out=sg[:, 0:n], in0=sg[:, 0:n], in1=ba[:, 0:n], op=ALU.mult), [sgB, bab], [sgB])
                    xb = xbuf(c0)
                    P.op("dve", lambda e, fo=fo, c0=c0, n=n: e.tensor_tensor(out=xs[:, fo, c0:c0 + n], in0=xs[:, fo, c0:c0 + n], in1=sg[:, 0:n], op=ALU.add), [sgB, xb], [xb])

    def gnorm(c0, n, gain, gainB, dst_fn, dstB):
        xb = xbuf(c0)
        NRM.rstd(lambda k: xs[:, k, c0:c0 + n], [xb], n, rstd[:, 0:n], rstdB)
        for k in range(8):
            if gain is None:
                P.op("dve", lambda e, k=k: e.tensor_tensor(out=dst_fn(k), in0=xs[:, k, c0:c0 + n], in1=rstd[:, 0:n], op=ALU.mult), [xb, rstdB], [dstB])
            else:
                P.op("dve", lambda e, k=k: e.scalar_tensor_tensor(out=dst_fn(k), in0=xs[:, k, c0:c0 + n], scalar=gain[:, k:k + 1], in1=rstd[:, 0:n], op0=ALU.mult, op1=ALU.mult),
                     [xb, rstdB, gainB], [dstB])

    def tile_pass(tt):
        c0 = 2 + tt * 512
        xi = tt % 2
        X = xh[xi]
        mixer(tt)
        if dbg < 2:
            P.dma("pool", lambda e, tt=tt, c0=c0: e.dma_start(out=xout[:, :, tt * 512:(tt + 1) * 512], in_=xs[:, :, c0:c0 + 512]), [xsB[tt]], [outB])
            return
        if tt == 0:
            gnorm(0, 2, gffn, smallB, lambda k: X[:, k, 0:2], xhH[xi])
        gnorm(c0, 512, gffn, smallB, lambda k: X[:, k, 2:514], xhB[xi])
        for j in range(NJ):
            tiles = []
            for half in range(2):
                w, wB = next_wt()
                WL.load(wsrc(wup, half * DFF + j * 128), w[:], wB, 8, 128, key=("up", half, j), first=(tt == 0))
                bank, bb = C.bank()
                hb, hbb = C.bank()
                for k in range(8):
                    P.op("pe", lambda e, k=k, bank=bank, w=w: e.matmul(bank[:, :], w[:, k, :], X[:, k, 2:514], start=(k == 0), stop=(k == 7)), [wB, xhB[xi]], [bb])
                n_ = half * NJ + j
                H = hsb[half]
                A = acc[half]
                if tt == 0:
                    for k in range(8):
                        P.op("pe", lambda e, k=k, hb=hb, w=w: e.matmul(hb[:, 0:2], w[:, k, :], X[:, k, 0:2], start=(k == 0), stop=(k == 7)), [wB, xhH[xi]], [hbb])
                P.op("act", lambda e, H=H, bank=bank: e.activation(out=H[:, 2:514], in_=bank[:, :], func=AF.Copy), [bb], [hsbB[half]])
                if tt == 0:
                    P.op("act", lambda e, H=H, hb=hb: e.activation(out=H[:, 0:2], in_=hb[:, 0:2], func=AF.Identity, scale=hmask[:, 0:1]), [hbb, smallB], [hsbB[half]])
                else:
                    P.op("pool", lambda e, H=H, n_=n_: e.tensor_copy(H[:, 0:2], hsave[:, n_, :]), [hhB[n_]], [hsbB[half]])
                if tt + 1 < NTT:
                    P.op("pool", lambda e, H=H, n_=n_: e.tensor_copy(hsave[:, n_, :], H[:, 512:514]), [hsbB[half]], [hhB[n_]])
                P.op("act", lambda e, H=H, A=A, n_=n_: e.activation(out=A[:, :], in_=H[:, 2:514], func=AF.Identity, scale=cw[:, 2, n_:n_ + 1], bias=cb[:, n_:n_ + 1]),
                     [hsbB[half], smallB], [accB[half]])
                if True:
                    P.op("dve", lambda e, H=H, A=A, n_=n_: e.scalar_tensor_tensor(out=A[:, :], in0=H[:, 1:513], scalar=cw[:, 1, n_:n_ + 1], in1=A[:, :], op0=ALU.mult, op1=ALU.add),
                         [hsbB[half], smallB, accB[half]], [accB[half]])
                    P.op("dve", lambda e, H=H, A=A, n_=n_: e.scalar_tensor_tensor(out=A[:, :], in0=H[:, 0:512], scalar=cw[:, 0, n_:n_ + 1], in1=A[:, :], op0=ALU.mult, op1=ALU.add),
                         [hsbB[half], smallB, accB[half]], [accB[half]])
            P.op("act", lambda e: e.activation(out=acc[0][:, :], in_=acc[0][:, :], func=AF.Silu), [accB[0]], [accB[0]])
            P.op("dve", lambda e, j=j: e.tensor_tensor(out=aT[:, j, :], in0=acc[0][:, :], in1=acc[1][:, :], op=ALU.mult), [accB[0], accB[1]], [aTB[j]])
        for fo in range(8):
            di = fo % 2
            for hh in range(2):
                WL.load(wdn.rearrange("(j p) c -> p j c", p=128)[:, hh * 11:(hh + 1) * 11, fo * 128:(fo + 1) * 128], wd[di][:, hh * 11:(hh + 1) * 11, :], wdB[di][hh], 11, 128, key=("dn", fo, hh), first=(tt == 0))
            bank, bb = C.bank()
            for j in range(NJ):
                P.op("pe", lambda e, j=j, bank=bank, di=di: e.matmul(bank[:, :], wd[di][:, j, :], aT[:, j, :], start=(j == 0), stop=(j == NJ - 1)), [wdB[di][j // 11], aTB[j]], [bb])
            P.op("dve", lambda e, fo=fo, bank=bank: e.tensor_tensor(out=xs[:, fo, c0:c0 + 512], in0=xs[:, fo, c0:c0 + 512], in1=bank[:, :], op=ALU.add), [bb, xsB[tt]], [xsB[tt]])
        if dbg < 3:
            P.dma("pool", lambda e, tt=tt, c0=c0: e.dma_start(out=xout[:, :, tt * 512:(tt + 1) * 512], in_=xs[:, :, c0:c0 + 512]), [xsB[tt]], [outB])
            return
        X3 = xh[xi]
        if tt + 1 < NTT:
            Xn = xh[1 - xi]
            P.op("pool", lambda e, X=X, Xn=Xn: e.tensor_copy(Xn[:, :, 0:2], X[:, :, 512:514]), [xhB[xi]], [xhH[1 - xi]])
        gnorm(c0, 512, None, None, lambda k: X3[:, k, 2:514], xhB[xi])
        WL.load(pT_d[:, :, tt * 512:(tt + 1) * 512], pT[:], pTB, 2, 512)
        for fo in range(8):
            w, wB = next_wt()
            WL.load(wsrc(wgate, fo * 128), w[:], wB, 8, 128, key=("gate", fo), first=(tt == 0))
            pi = fo % 2
            WL.load(wproj.rearrange("(k p) c -> p k c", p=128)[:, :, fo * 128:(fo + 1) * 128], wpj[pi][:], wpjB[pi], 2, 128, key=("proj", fo), first=(tt == 0))
            bg, bgb = C.bank()
            bp, bpb = C.bank()
            for k in range(8):
                P.op("pe", lambda e, k=k, bg=bg, w=w: e.matmul(bg[:, :], w[:, k, :], X3[:, k, 2:514], start=(k == 0), stop=(k == 7)), [wB, xhB[xi]], [bgb])
            for k in range(2):
                P.op("pe", lambda e, k=k, bp=bp, pi=pi: e.matmul(bp[:, :], wpj[pi][:, k, :], pT[:, k, :], start=(k == 0), stop=(k == 1)), [wpjB[pi], pTB], [bpb])
            P.op("act", lambda e, bg=bg: e.activation(out=sg[:, :], in_=bg[:, :], func=AF.Sigmoid), [bgb], [sgB])
            P.op("dve", lambda e, bp=bp: e.tensor_tensor(out=sg[:, :], in0=sg[:, :], in1=bp[:, :], op=ALU.mult), [sgB, bpb], [sgB])
            P.op("dve", lambda e, fo=fo: e.tensor_tensor(out=xs[:, fo, c0:c0 + 512], in0=xs[:, fo, c0:c0 + 512], in1=sg[:, :], op=ALU.add), [sgB, xsB[tt]], [xsB[tt]])
        if xout is not None:
            P.dma("pool", lambda e, tt=tt, c0=c0: e.dma_start(out=xout[:, :, tt * 512:(tt + 1) * 512], in_=xs[:, :, c0:c0 + 512]), [xsB[tt]], [outB])
        if want_hn:
            gnorm(c0, 512, gnext, smallB, lambda k: hnt[:, k, :], hnB)
            P.dma("pool", lambda e, tt=tt: e.dma_start(out=hn_d.rearrange("(k p) t -> p k t", p=128)[:, :, tt * 512:(tt + 1) * 512], in_=hnt[:, :, :]), [hnB], [hnB_d])
    for tt in range(NTT):
        tile_pass(tt)
    C.release(m0)


def emit_pre_even(C, io):
    P = C.P
    m0 = C.mark()
    hn_d = io["hn_d"]
    win = io["win"]; vgain_d = io["vgain"]; wsT_d = io["wsT"]; tri_d = io["tri"]; bsb_d = io["bsb"]
    qg_d = io["qg"]; kg_d = io["kg"]; bf_d = io["bfg"]
    ya_d = io["ya_d"]; qks = io["qks"]; Vs = io["Vs"]; lss = io["lss"]
    xs = io["xs"]; xsB = io["xsB"]

    ones, onesB, eps, epsB = C.K
    NRM = Norm(C, ones, onesB, eps, epsB)
    WL = WLoader(C, 8 * 128, nbuf=3)
    onef = C.sb("onef", [128, 1]); onefB = C.buf("onef")
    P.op("pool", lambda e: e.memset(onef[:], 1.0), [], [onefB])

    X = C.sb("X", [128, 8, TPC], BF16); XB = C.buf("X")
    for k in range(8):
        P.dma("sp", lambda e, k=k: e.dma_start(out=X[:, k, :], in_=hn_d[k * 128:(k + 1) * 128, :]), [io["hn_dB"]], [XB])
    smallB = C.buf("small")
    vgain = C.sb("vgain", [128, 512]); wsT = C.sb("wsT", [128, 4, 128]); tri = C.sb("tri", [128, 128]); bsb = C.sb("bsb", [128, 512])
    qg = C.sb("qg", [64, 1]); kg = C.sb("kg", [64, 1]); bfg = C.sb("bfg", [8, 1])
    for dst, src in ((vgain, vgain_d), (wsT, wsT_d), (tri, tri_d), (bsb, bsb_d), (qg, qg_d), (kg, kg_d), (bfg, bf_d)):
        load_f32(C, dst[:], src, smallB, "pool")
    wsb = C.sb("wsb", [128, 4, 128], BF16); wsbB = C.buf("wsb")
    P.op("dve", lambda e: e.tensor_tensor(out=wsb[:], in0=wsT[:], in1=tri[:, :].unsqueeze(1).to_broadcast([128, 4, 128]), op=ALU.mult), [smallB], [wsbB])
    qgs = C.sb("qgs", [64, 1]); negb = C.sb("negb", [8, 1]); sm2B = C.buf("sm2")
    P.op("dve", lambda e: e.tensor_scalar(out=qgs[:], in0=qg[:], scalar1=0.125, scalar2=None, op0=ALU.mult), [smallB], [sm2B])
    P.op("dve", lambda e: e.tensor_scalar(out=negb[:], in0=bfg[:], scalar1=-1.0, scalar2=None, op0=ALU.mult), [smallB], [sm2B])

    wt = [C.sb("wt%d" % i, [128, 8, 128], BF16) for i in range(3)]
    wtB = [C.buf("wt") for _ in range(3)]
    wti = [0]

    def next_wt():
        i = wti[0]
        wti[0] = (i + 1) % 3
        return wt[i], wtB[i]

    def wsrc(c0, ncol=128):
        return win.rearrange("(k p) c -> p k c", p=128)[:, :, c0:c0 + ncol]

    wv = C.sb("wv", [128, 8, 512], BF16); wvB = [C.buf("wv") for _ in range(4)]

    def load_wv(col0):
        for c in range(4):
            WL.load(wsrc(col0 + c * 128), wv[:, :, c * 128:(c + 1) * 128], wvB[c], 8, 128)

    NQB = 3
    qs = [C.sb("qs", [64, 512]) for _ in range(NQB)]; qsB = [C.buf("qs") for _ in range(NQB)]
    sqq = [C.sb("sqq", [64, 512], BF16) for _ in range(NQB)]; sqqB = [C.buf("sqq") for _ in range(NQB)]
    rq = [C.sb("rq", [64, 512]) for _ in range(2)]; rqB = [C.buf("rq") for _ in range(2)]
    qo = [C.sb("qo%d" % i, [64, 512], BF16) for i in range(2)]; qoB = [C.buf("qo") for _ in range(2)]
    its = []
    for (col0, gcol, roff) in ((1024, qgs, 0), (1536, kg, 64)):
        for h2 in range(4):
            for hh in range(2):
                for tt in range(NTT):
                    its.append((col0, gcol, roff, h2, hh, tt))
    wcur = {}

    def qk_s1(i):
        col0, gcol, roff, h2, hh, tt = its[i]
        if hh == 0 and tt == 0:
            w, wB = next_wt()
            WL.load(wsrc(col0 + h2 * 128), w[:], wB, 8, 128)
            wcur[(col0, h2)] = (w, wB)
        w, wB = wcur[(col0, h2)]
        bank, bb = C.bank()
        for k in range(8):
            P.op("pe", lambda e, k=k: e.matmul(bank[0:64, :], w[:, k, hh * 64:(hh + 1) * 64], X[:, k, tt * 512:(tt + 1) * 512], start=(k == 0), stop=(k == 7)), [wB, XB], [bb])
        q = i % NQB
        P.op("dve", lambda e: e.tensor_copy(qs[q][:, :], bank[0:64, :]), [bb], [qsB[q]])
        P.op("pool", lambda e: e.tensor_tensor(out=sqq[q][:, :], in0=qs[q][:, :], in1=qs[q][:, :], op=ALU.mult), [qsB[q]], [sqqB[q]])

    def qk_s2(i):
        col0, gcol, roff, h2, hh, tt = its[i]
        h = 2 * h2 + hh
        q = i % NQB
        r = i % 2
        bank, bb = C.bank()
        P.op("pe", lambda e: e.matmul(bank[0:64, :], ones[0:64, 0:64], sqq[q][:, :], start=True, stop=True), [sqqB[q], onesB], [bb])
        P.op("act", lambda e: e.activation(out=rq[r][:, :], in_=bank[0:64, :], func=AF.Ln, scale=1.0 / 64.0, bias=eps[0:64, 0:1]), [bb, epsB], [rqB[r]])
        P.op("act", lambda e: e.activation(out=rq[r][:, :], in_=rq[r][:, :], func=AF.Exp, scale=-0.5), [rqB[r]], [rqB[r]])
        P.op("dve", lambda e: e.scalar_tensor_tensor(out=qo[r][:, :], in0=qs[q][:, :], scalar=gcol[:, 0:1], in1=rq[r][:, :], op0=ALU.mult, op1=ALU.mult),
             [qsB[q], rqB[r], smallB, sm2B], [qoB[r]])
        P.dma("pool", lambda e: e.dma_start(out=qks[h * 128 + roff:h * 128 + roff + 64, tt * 512:(tt + 1) * 512], in_=qo[r][:, :]), [qoB[r]], [io["qksB"]])

    qk_s1(0)
    for i in range(len(its)):
        if i + 1 < len(its):
            qk_s1(i + 1)
        qk_s2(i)

    allgather(C, qks, io['qksB'], io['qkg'], io['qkgB'])
    w, wB = next_wt()
    WL.load(wsrc(2560, 8), w[:, :, 0:8], wB, 8, 8)
    fe = C.sb("fe", [8, 512]); feB = C.buf("fe")
    for tt in range(NTT):
        bank, bb = C.bank()
        for k in range(8):
            P.op("pe", lambda e, k=k, bank=bank, w=w, tt=tt: e.matmul(bank[0:8, :], w[:, k, 0:8], X[:, k, tt * 512:(tt + 1) * 512], start=(k == 0), stop=(k == 7)), [wB, XB], [bb])
        P.op("act", lambda e, bank=bank: e.activation(out=fe[:, :], in_=bank[0:8, :], func=AF.Exp, scale=-1.0, bias=negb[:, 0:1]), [bb, sm2B], [feB])
        P.op("act", lambda e: e.activation(out=fe[:, :], in_=fe[:, :], func=AF.Ln, scale=1.0, bias=onef[0:8, 0:1]), [feB, onefB], [feB])
        P.op("dve", lambda e: e.tensor_scalar(out=fe[:, :], in0=fe[:, :], scalar1=-1.0, scalar2=None, op0=ALU.mult), [feB], [feB])
        P.dma("pool", lambda e, tt=tt: e.dma_start(out=lss[:, tt * 512:(tt + 1) * 512], in_=fe[:, :]), [feB], [io["lssB"]])

    allgather(C, lss, io['lssB'], io['lsg'], io['lsgB'])
    load_wv(2048)
    vt = [C.sb("vt%d" % i, [128, 512], BF16) for i in range(2)]; vtB = [C.buf("vt") for _ in range(2)]
    for c in range(TPC // 128):
        bank, bb = C.bank()
        for k in range(8):
            P.op("pe", lambda e, k=k, bank=bank, c=c: e.matmul(bank[:, :], X[:, k, c * 128:(c + 1) * 128], wv[:, k, :], start=(k == 0), stop=(k == 7)), wvB + [XB], [bb])
        vi = c % 2
        P.op("act", lambda e, bank=bank, vi=vi: e.activation(out=vt[vi][:, :], in_=bank[:, :], func=AF.Copy), [bb], [vtB[vi]])
        P.dma("pool", lambda e, vi=vi, c=c: e.dma_start(out=Vs.rearrange("(h t) d -> t h d", h=8)[c * 128:(c + 1) * 128, :, :], in_=vt[vi][:, :].rearrange("p (h d) -> p h d", h=8)), [vtB[vi]], [io["VsB"]])
    allgather(C, Vs, io['VsB'], io['Vg'], io['VgB'])
    gu = C.sb("gu", [128, 4, TPC], BF16); guB = [C.buf("gu") for _ in range(4)]
    for ft in range(4):
        w, wB = next_wt()
        WL.load(wsrc(ft * 128), w[:], wB, 8, 128)
        for tt in range(NTT):
            bank, bb = C.bank()
            for k in range(8):
                P.op("pe", lambda e, k=k, bank=bank, w=w, tt=tt: e.matmul(bank[:, :], w[:, k, :], X[:, k, tt * 512:(tt + 1) * 512], start=(k == 0), stop=(k == 7)), [wB, XB], [bb])
            P.op("act", lambda e, bank=bank, ft=ft, tt=tt: e.activation(out=gu[:, ft, tt * 512:(tt + 1) * 512], in_=bank[:, :], func=AF.Gelu), [bb], [guB[ft]])

    load_wv(512)
    gv = C.sb("gv", [128, 512]); gvB = C.buf("gv")
    sqv = C.sb("sqv", [128, 512]); sqvB = C.buf("sqv")
    ss = C.sb("ss", [128, 4]); ssB = C.buf("ss")
    vn = C.sb("vn", [128, 512], BF16); vnB = C.buf("vn")
    tmp = C.sb("tmp", [128, 512]); tmpB = C.buf("tmp")
    ya = C.sb("ya", [128, 4, TPC], BF16); yaB = C.buf("ya")
    for c in range(TPC // 128):
        bank, bb = C.bank()
        for k in range(8):
            P.op("pe", lambda e, k=k, bank=bank, c=c: e.matmul(bank[:, :], X[:, k, c * 128:(c + 1) * 128], wv[:, k, :], start=(k == 0), stop=(k == 7)), wvB + [XB], [bb])
        P.op("act", lambda e, bank=bank: e.activation(out=gv[:], in_=bank[:, :], func=AF.Gelu), [bb], [gvB])
        P.op("act", lambda e: e.activation(out=sqv[:], in_=gv[:], func=AF.Square), [gvB], [sqvB])
        P.op("dve", lambda e: e.tensor_reduce(out=ss[:, :], in_=sqv[:, :].rearrange("p (g c) -> p g c", g=4), axis=AX.X, op=ALU.add), [sqvB], [ssB])
        P.op("act", lambda e: e.activation(out=ss[:, :], in_=ss[:, :], func=AF.Sqrt, scale=1.0 / 128, bias=eps[:, 0:1]), [ssB, epsB], [ssB])
        P.op("dve", lambda e: e.reciprocal(ss[:, :], ss[:, :]), [ssB], [ssB])
        for g in range(4):
            P.op("dve", lambda e, g=g: e.scalar_tensor_tensor(out=vn[:, g * 128:(g + 1) * 128], in0=gv[:, g * 128:(g + 1) * 128], scalar=ss[:, g:g + 1], in1=vgain[:, g * 128:(g + 1) * 128], op0=ALU.mult, op1=ALU.mult),
                 [gvB, ssB, smallB], [vnB])
        b2, b2b = C.bank()
        for g in range(4):
            P.op("pe", lambda e, g=g, b2=b2: e.matmul(b2[:, g * 128:(g + 1) * 128], vn[:, g * 128:(g + 1) * 128], wsb[:, g, :], start=True, stop=True), [vnB, wsbB], [b2b])
        P.op("dve", lambda e, b2=b2: e.tensor_tensor(out=tmp[:], in0=b2[:, :], in1=bsb[:], op=ALU.add), [b2b, smallB], [tmpB])
        P.op("dve", lambda e, c=c: e.tensor_tensor(out=ya[:, :, c * 128:(c + 1) * 128], in0=tmp[:, :].rearrange("p (g c) -> p g c", g=4), in1=gu[:, :, c * 128:(c + 1) * 128], op=ALU.mult),
             [tmpB] + guB, [yaB])
    for ft in range(4):
        P.dma("pool", lambda e, ft=ft: e.dma_start(out=ya_d[:, ft, :], in_=ya[:, ft, :]), [yaB], [io["ya_dB"]])
    hs = C.sb("hs", [128, 24]); hsB = C.buf("hs")
    P.op("dve", lambda e: e.tensor_copy(hs[:, 0:16].rearrange("p (k t) -> p k t", k=8), xs[:, :, TPC:TPC + 2]), [xsB[NTT - 1]], [hsB])
    P.op("dve", lambda e: e.tensor_copy(hs[:, 16:24].rearrange("p (k t) -> p k t", k=4), ya[:, :, TPC - 2:TPC]), [yaB, hsB], [hsB])
    P.dma("pool", lambda e: e.dma_start(out=io["hsend"], in_=hs[:, :]), [hsB], [io["hsendB"]])

    allgather(C, io["hsend"], io["hsendB"], io["hrecv"], io["hrecvB"])
    cidv = lambda: P.vals["sp"]["cid"]
    qkg = io["qkg"]; Vg = io["Vg"]; lsg = io["lsg"]; mineB = io["mineB"]
    qk_mine = io["qk_mine"]; V_mine = io["V_mine"]; ls_mine = io["ls_mine"]
    P.dma("sp", lambda e: e.dma_start(out=qk_mine.rearrange("q (r t) -> q r t", r=NCORES), in_=qkg.rearrange("(r q) t -> q r t", r=NCORES)[bass.ds(cidv() * 128, 128), :, :]), [io["qkgB"]], [mineB])
    P.dma("sp", lambda e: e.dma_start(out=V_mine.rearrange("(r t) d -> t r d", r=NCORES), in_=Vg.rearrange("(r q) d -> q r d", r=NCORES)[bass.ds(cidv() * TPC, TPC), :, :]), [io["VgB"]], [mineB])
    P.dma("sp", lambda e: e.dma_start(out=ls_mine.rearrange("o (r t) -> o r t", r=NCORES), in_=lsg.rearrange("(r h) t -> h r t", r=NCORES)[bass.ds(cidv(), 1), :, :]), [io["lsgB"]], [mineB])
    C.release(m0)


def emit_attn(C, io, nqt=SEQ // 512):
    P = C.P
    m0 = C.mark()
    S_ = nqt * 512
    nkt_all = S_ // 128
    qkg = io["qkg"]; Vg = io["Vg"]; lsg = io["lsg"]
    qkgB = io["qkgB"]; VgB = io["VgB"]; lsgB = io["lsgB"]
    osend = io["osend"]; osendB = io["osendB"]
    cidv = lambda: P.vals["sp"]["cid"]
    qk_mine = io["qk_mine"]; V_mine = io["V_mine"]; ls_mine = io["ls_mine"]; mineB = io["mineB"]

    KA = C.sb("KA", [128, S_], BF16)
    VA = C.sb("VA", [128, nkt_all, 128], BF16)
    PW = 512
    NPC = S_ // PW
    kaB = [C.buf("KA") for _ in range(NPC)]
    vaB = [C.buf("VA") for _ in range(NPC)]
    smallB = C.buf("small")
    mask = C.sb("mask", [128, 4, 512], BF16); ident = C.sb("ident", [128, 128], BF16); mcol = C.sb("mcol", [128, 4])
    P.dma("pool", lambda e: e.dma_start(out=mask[:], in_=io["maskneg"]), [], [smallB])
    P.dma("pool", lambda e: e.dma_start(out=ident[:], in_=io["ident"]), [], [smallB])
    P.dma("pool", lambda e: e.dma_start(out=mcol[:], in_=io["mcols"]), [], [smallB])
    zt = C.sb("zt", [64, 2], BF16); ztB = C.buf("zt")
    P.op("pool", lambda e: e.memset(zt[:], 0.0), [], [ztB])
    P.dma("pool", lambda e: e.dma_start(out=osend[:, 0:2], in_=zt[:, :]), [ztB], [osendB])
    lsb = C.sb("lsb", [128, PW]); lsB = C.buf("ls")
    cc = C.sb("cc", [128, PW]); ccB = C.buf("cc")
    c1 = C.sb("c1", [128, PW], BF16); c2 = C.sb("c2", [128, PW], BF16); c3 = C.sb("c3", [128, PW], BF16)
    r1 = C.sb("r1", [128, PW]); r2 = C.sb("r2", [128, PW]); tq = C.sb("tq", [128, PW])
    pcB = C.buf("pc")
    onesr = C.sb("onesr", [128, PW]); onesrB = C.buf("onesr")
    carry = C.sb("carry", [128, 1]); carryB = C.buf("carry")
    P.op("pool", lambda e: e.memset(onesr[:], 1.0), [], [onesrB])
    P.op("pool", lambda e: e.memset(carry[:], 0.0), [], [carryB])
    R = slice(64, 128)
    for pc in range(NPC):
        a, b = pc * PW, (pc + 1) * PW
        r = a // TPC
        la = a % TPC
        k0, k1 = a // 128, b // 128
        P.dma("sp", lambda e, a=a, b=b: e.dma_start(out=KA[0:64, a:b], in_=qk_mine[64:128, a:b]), [mineB], [kaB[pc]])
        P.op("pool", lambda e, k0=k0, k1=k1: e.memset(VA[:, k0:k1, 64:128], 1.0), [], [vaB[pc]])
        P.dma("sp", lambda e, k0=k0, k1=k1: e.dma_start(out=VA[:, k0:k1, 0:64], in_=V_mine.rearrange("(kt p) d -> p kt d", p=128)[:, k0:k1, :]), [mineB], [vaB[pc]])
        P.dma("sp", lambda e, a=a, b=b: e.dma_start(out=lsb[R, :], in_=ls_mine[0:1, a:b].to_broadcast([64, PW])), [mineB], [lsB])
        P.op("dve", lambda e: e.tensor_tensor_scan(out=cc[R, :], data0=onesr[R, :], data1=lsb[R, :], initial=carry[R, 0:1], op0=ALU.mult, op1=ALU.add),
             [onesrB, lsB, carryB], [ccB])
        P.op("dve", lambda e: e.tensor_copy(carry[R, 0:1], cc[R, PW - 1:PW]), [ccB], [carryB])
        P.op("dve", lambda e: e.tensor_copy(c1[R, :], cc[R, :]), [ccB], [pcB])
        P.op("dve", lambda e: e.tensor_tensor(out=r1[R, :], in0=cc[R, :], in1=c1[R, :], op=ALU.subtract), [ccB, pcB], [pcB])
        P.op("dve", lambda e: e.tensor_copy(c2[R, :], r1[R, :]), [pcB], [pcB])
        P.op("dve", lambda e: e.tensor_tensor(out=r2[R, :], in0=r1[R, :], in1=c2[R, :], op=ALU.subtract), [pcB], [pcB])
        P.op("dve", lambda e: e.tensor_copy(c3[R, :], r2[R, :]), [pcB], [pcB])
        P.op("dve", lambda e: e.tensor_scalar(out=tq[R, :], in0=c1[R, :], scalar1=mcol[R, 0:1], scalar2=mcol[R, 1:2], op0=ALU.mult, op1=ALU.add), [pcB, smallB], [pcB])
        P.op("dve", lambda e: e.scalar_tensor_tensor(out=tq[R, :], in0=c2[R, :], scalar=mcol[R, 2:3], in1=tq[R, :], op0=ALU.mult, op1=ALU.add), [pcB, smallB], [pcB])
        P.op("dve", lambda e, a=a, b=b: e.scalar_tensor_tensor(out=KA[R, a:b], in0=c3[R, :], scalar=mcol[R, 3:4], in1=tq[R, :], op0=ALU.mult, op1=ALU.add), [pcB, smallB], [kaB[pc]])

    NPT = 4
    PT = [C.sb("PT", [128, 512], BF16) for i in range(NPT)]
    PTB = [C.buf("PT") for _ in range(NPT)]
    QAt = [C.sb("QAt", [128, 512], BF16) for i in range(2)]
    QAtB = [C.buf("QAt") for _ in range(2)]
    rec = [C.sb("rec", [128, 512]) for i in range(2)]; recB = [C.buf("rec") for _ in range(2)]
    recl = [C.sb("recl", [64, 512]) for i in range(2)]; reclB = [C.buf("recl") for _ in range(2)]
    ot = [C.sb("ot", [64, 512], BF16) for i in range(2)]; otB = [C.buf("ot") for _ in range(2)]
    NSB = 6

    def sbank(i):
        j = i % NSB
        return C.ps[:, j * 512:(j + 1) * 512], C.psb[j]

    def obank(qt):
        j = 6 + (qt % 2)
        return C.ps[:, j * 512:(j + 1) * 512], C.psb[j]

    pairs = [(qt, kt) for qt in range(nqt) for kt in range(4 * qt + 4)]
    LOOK = 2

    def load_q(qt):
        qi = qt % 2
        a = qt * 512
        r = a // TPC
        la = a % TPC
        P.dma("sp", lambda e: e.dma_start(out=QAt[qi][0:64, :], in_=qk_mine[0:64, a:a + 512]), [mineB], [QAtB[qi]])
        P.dma("sp", lambda e: e.dma_start(out=QAt[qi][64:96, :], in_=KA[96:128, a:a + 512]), [kaB[a // PW]], [QAtB[qi]])

    def issue_S(i):
        qt, kt = pairs[i]
        if kt == 0:
            load_q(qt)
        S, SB = sbank(i)
        diag = kt >= 4 * qt
        kp = (kt * 128) // PW
        qi = qt % 2
        P.op("pe", lambda e: e.matmul(S[:, :], KA[0:96, kt * 128:(kt + 1) * 128], QAt[qi][0:96, :], start=True, stop=(not diag)),
             [kaB[kp], QAtB[qi]], [SB])
        if diag:
            P.op("pe", lambda e: e.matmul(S[:, :], ident[:, :], mask[:, kt - 4 * qt, :], start=False, stop=True), [smallB], [SB])
        pi = i % NPT
        P.op("act", lambda e: e.activation(out=PT[pi][:, :], in_=S[:, :], func=AF.Exp), [SB], [PTB[pi]])

    def issue_PV(i):
        qt, kt = pairs[i]
        nkt = 4 * qt + 4
        O, OB = obank(qt)
        pi = i % NPT
        kp = (kt * 128) // PW
        P.op("pe", lambda e: e.matmul(O[:, :], VA[:, kt, :], PT[pi][:, :], start=(kt == 0), stop=(kt == nkt - 1)), [vaB[kp], PTB[pi]], [OB])
        if kt == nkt - 1:
            ri = qt % 2
            P.op("dve", lambda e: e.reciprocal(rec[ri][64:128, :], O[64:128, :]), [OB], [recB[ri]])
            P.dma("pool", lambda e: e.dma_start(out=recl[ri][0:64, :], in_=rec[ri][64:128, :]), [recB[ri]], [reclB[ri]])
            P.op("dve", lambda e: e.tensor_tensor(out=ot[ri][:, :], in0=O[0:64, :], in1=recl[ri][:, :], op=ALU.mult), [OB, reclB[ri]], [otB[ri]])
            P.dma("pool", lambda e: e.dma_start(out=osend[:, 2 + qt * 512:2 + (qt + 1) * 512], in_=ot[ri][:, :]), [otB[ri]], [osendB])

    for i in range(len(pairs) + LOOK):
        if i < len(pairs):
            issue_S(i)
        if i - LOOK >= 0:
            issue_PV(i - LOOK)
    C.release(m0)


def attn_consts():
    kk = np.arange(128)[:, None, None]
    dk = np.arange(4)[None, :, None]
    qq = np.arange(512)[None, None, :]
    maskneg = np.where(dk * 128 + kk <= qq, 0.0, -30000.0).astype(np.float32).astype(NPBF)
    ident = np.eye(128, dtype=np.float32).astype(NPBF)
    mc = np.zeros((128, 4), np.float32)
    mc[67, 0] = -1; mc[96, 0] = 1
    mc[64:67, 1] = 1; mc[99:102, 1] = 1
    mc[68, 2] = -1; mc[97, 2] = 1
    mc[69, 3] = -1; mc[98, 3] = 1
    return dict(maskneg=np.ascontiguousarray(maskneg), ident=ident, mcols=mc)


TWO_PI = 6.283185307179586
MAGIC = 12582912.0


def emit_s5(C, io, nblk=SEQ // 512):
    P = C.P
    m0 = C.mark()
    S_ = nblk * 512
    hng = io["hng"]; hngB = io["hngB"]
    ycs = io["ycs"]; ycsB = io["ycsB"]
    cidv = lambda: P.vals["sp"]["cid"]
    smallB = C.buf("small")
    are = C.sb("are", [128, 4]); aim = C.sb("aim", [128, 4]); ldt = C.sb("ldt", [128, 4])
    bre = C.sb("bre", [128, 4, 16]); bim = C.sb("bim", [128, 4, 16]); cre = C.sb("cre", [128, 4, 16]); cim = C.sb("cim", [128, 4, 16])
    identf = C.sb("identf", [128, 128]); jidx = C.sb("jidx", [128, 128])
    for dst, key in ((are, "are"), (aim, "aim"), (ldt, "ldt"), (bre, "bre"), (bim, "bim"), (cre, "creT"), (cim, "cimT"), (identf, "identf"), (jidx, "jidx")):
        load_f32(C, dst[:], io[key], smallB, "pool")
    zt = C.sb("zt", [128, 2], BF16); ztB = C.buf("zt")
    P.op("pool", lambda e: e.memset(zt[:], 0.0), [], [ztB])
    P.dma("pool", lambda e: e.dma_start(out=ycs[:, 0:2], in_=zt[:, :]), [ztB], [ycsB])
    uT = C.sb("uT", [128, S_], BF16)
    NPC = max(1, S_ // TPC)
    PW = S_ // NPC
    uB = [C.buf("uT") for _ in range(NPC)]
    P.dma("sp", lambda e: e.dma_start(out=uT[:, :].rearrange("p (r t) -> p r t", r=NCORES), in_=hng.rearrange("(r q) t -> q r t", r=NCORES)[bass.ds(cidv() * 128, 128), :, :]), [hngB], uB)

    kB = C.buf("k")
    n_s = [0]

    def st_(shape, dt=F32):
        n_s[0] += 1
        return C.sb("k%d" % n_s[0], shape, dt)

    def dve(fn, extra=()):
        P.op("dve", fn, [kB, smallB] + list(extra), [kB])

    def act(fn):
        P.op("act", fn, [kB, smallB], [kB])

    halfpi = st_([128, 1]); zero = st_([128, 1])
    P.op("pool", lambda e: e.memset(halfpi[:], TWO_PI / 4), [], [kB])
    P.op("pool", lambda e: e.memset(zero[:], 0.0), [kB], [kB])

    def sincos(th, n, cos_out, sin_out):
        t = st_([128, n]); kk = st_([128, n]); tr = st_([128, n]); ab = st_([128, n])
        dve(lambda e: e.tensor_scalar(out=t[:], in0=th, scalar1=1.0 / TWO_PI, scalar2=MAGIC, op0=ALU.mult, op1=ALU.add))
        dve(lambda e: e.tensor_scalar(out=kk[:], in0=t[:], scalar1=-MAGIC, scalar2=None, op0=ALU.add))
        dve(lambda e: e.scalar_tensor_tensor(out=tr[:], in0=kk[:], scalar=-6.28125, in1=th, op0=ALU.mult, op1=ALU.add))
        dve(lambda e: e.scalar_tensor_tensor(out=tr[:], in0=kk[:], scalar=-(TWO_PI - 6.28125), in1=tr[:], op0=ALU.mult, op1=ALU.add))
        dve(lambda e: e.tensor_scalar(out=tr[:], in0=tr[:], scalar1=TWO_PI / 2, scalar2=-TWO_PI / 2, op0=ALU.min, op1=ALU.max))
        act(lambda e: e.activation(out=sin_out, in_=tr[:], func=AF.Sin, bias=zero[:, 0:1]))
        dve(lambda e: e.scalar_tensor_tensor(out=ab[:], in0=tr[:], scalar=-1.0, in1=tr[:], op0=ALU.mult, op1=ALU.max))
        act(lambda e: e.activation(out=cos_out, in_=ab[:], func=AF.Sin, scale=-1.0, bias=halfpi[:, 0:1]))

    dt = st_([128, 4]); mag = st_([128, 4]); th = st_([128, 4]); cth = st_([128, 4]); sth = st_([128, 4])
    act(lambda e: e.activation(out=dt[:], in_=ldt[:], func=AF.Exp))
    dve(lambda e: e.tensor_tensor(out=mag[:], in0=are[:], in1=dt[:], op=ALU.mult))
    act(lambda e: e.activation(out=mag[:], in_=mag[:], func=AF.Exp))
    dve(lambda e: e.tensor_tensor(out=th[:], in0=aim[:], in1=dt[:], op=ALU.mult))
    sincos(th[:], 4, cth[:], sth[:])
    abr = st_([128, 4]); abi = st_([128, 4]); den = st_([128, 4]); cr = st_([128, 4]); ci = st_([128, 4]); t0 = st_([128, 4]); t1 = st_([128, 4])
    dve(lambda e: e.tensor_tensor(out=abr[:], in0=mag[:], in1=cth[:], op=ALU.mult))
    dve(lambda e: e.tensor_tensor(out=abi[:], in0=mag[:], in1=sth[:], op=ALU.mult))
    dve(lambda e: e.tensor_tensor(out=den[:], in0=are[:], in1=are[:], op=ALU.mult))
    dve(lambda e: e.tensor_tensor(out=t0[:], in0=aim[:], in1=aim[:], op=ALU.mult))
    dve(lambda e: e.tensor_tensor(out=den[:], in0=den[:], in1=t0[:], op=ALU.add))
    dve(lambda e: e.reciprocal(den[:], den[:]))
    dve(lambda e: e.tensor_scalar(out=t1[:], in0=abr[:], scalar1=-1.0, scalar2=None, op0=ALU.add))
    dve(lambda e: e.tensor_tensor(out=cr[:], in0=t1[:], in1=are[:], op=ALU.mult))
    dve(lambda e: e.tensor_tensor(out=t0[:], in0=abi[:], in1=aim[:], op=ALU.mult))
    dve(lambda e: e.tensor_tensor(out=cr[:], in0=cr[:], in1=t0[:], op=ALU.add))
    dve(lambda e: e.tensor_tensor(out=cr[:], in0=cr[:], in1=den[:], op=ALU.mult))
    dve(lambda e: e.tensor_tensor(out=ci[:], in0=abi[:], in1=are[:], op=ALU.mult))
    dve(lambda e: e.tensor_tensor(out=t0[:], in0=t1[:], in1=aim[:], op=ALU.mult))
    dve(lambda e: e.tensor_tensor(out=ci[:], in0=ci[:], in1=t0[:], op=ALU.subtract))
    dve(lambda e: e.tensor_tensor(out=ci[:], in0=ci[:], in1=den[:], op=ALU.mult))

    BT = [[C.sb("BT%d_%d" % (ri, st), [128, 128], BF16) for st in range(4)] for ri in range(2)]
    CT = [[C.sb("CT%d_%d" % (ri, st), [128, 128], BF16) for st in range(4)] for ri in range(2)]
    bfull = st_([128, 128]); tb = st_([128, 16])
    for st in range(4):
        for ri in range(2):
            dve(lambda e: e.memset(bfull[:], 0.0))
            for hf in range(2):
                rows = slice(hf * 64, hf * 64 + 64)
                cs = (2 * st + hf) * 16
                if ri == 0:
                    dve(lambda e, rows=rows, st=st: e.tensor_scalar(out=tb[rows, :], in0=bim[rows, st, :], scalar1=ci[rows, st:st + 1], scalar2=None, op0=ALU.mult))
                    dve(lambda e, rows=rows, st=st, cs=cs: e.scalar_tensor_tensor(out=bfull[rows, cs:cs + 16], in0=bre[rows, st, :], scalar=cr[rows, st:st + 1], in1=tb[rows, :], op0=ALU.mult, op1=ALU.subtract))
                else:
                    dve(lambda e, rows=rows, st=st: e.tensor_scalar(out=tb[rows, :], in0=bre[rows, st, :], scalar1=ci[rows, st:st + 1], scalar2=None, op0=ALU.mult))
                    dve(lambda e, rows=rows, st=st, cs=cs: e.scalar_tensor_tensor(out=bfull[rows, cs:cs + 16], in0=bim[rows, st, :], scalar=cr[rows, st:st + 1], in1=tb[rows, :], op0=ALU.mult, op1=ALU.add))
            bank, bb = C.bank()
            P.op("pe", lambda e, bank=bank: e.transpose(bank[:, 0:128], bfull[:, :], identf[:, :]), [kB, smallB], [bb])
            P.op("act", lambda e, bank=bank, ri=ri, st=st: e.activation(out=BT[ri][st][:, :], in_=bank[:, 0:128], func=AF.Copy), [bb], [kB])
            dve(lambda e, ri=ri, st=st: e.memset(CT[ri][st][:, :], 0.0))
            for hf in range(2):
                rows = slice(hf * 64, hf * 64 + 64)
                cs = (2 * st + hf) * 16
                srcc = cre if ri == 0 else cim
                sgn = 1.0 if ri == 0 else -1.0
                dve(lambda e, rows=rows, st=st, cs=cs, srcc=srcc, sgn=sgn, ri=ri: e.tensor_scalar(out=CT[ri][st][rows, cs:cs + 16], in0=srcc[rows, st, :], scalar1=sgn, scalar2=None, op0=ALU.mult))

    phi = st_([128, 512]); cosT = st_([128, 4, 128]); sinT = st_([128, 4, 128]); rt = st_([128, 4, 128])
    onesf = st_([128, 128])
    dve(lambda e: e.memset(onesf[:], 1.0))
    for st in range(4):
        dve(lambda e, st=st: e.tensor_scalar(out=phi[:, st * 128:(st + 1) * 128], in0=jidx[:, :], scalar1=th[:, st:st + 1], scalar2=None, op0=ALU.mult))
        dve(lambda e, st=st: e.tensor_scalar(out=rt[:, st, :], in0=onesf[:, :], scalar1=mag[:, st:st + 1], scalar2=None, op0=ALU.mult))
    sincos(phi[:], 512, cosT[:].rearrange("p a b -> p (a b)"), sinT[:].rearrange("p a b -> p (a b)"))
    thG = st_([128, 4]); Gr = st_([128, 4]); Gi = st_([128, 4])
    dve(lambda e: e.tensor_scalar(out=thG[:], in0=th[:], scalar1=128.0, scalar2=None, op0=ALU.mult))
    sincos(thG[:], 4, Gr[:], Gi[:])

    cosb = C.sb("cosb", [128, 4, 128], BF16); sinb = C.sb("sinb", [128, 4, 128], BF16)
    dve(lambda e: e.tensor_copy(cosb[:], cosT[:]))
    dve(lambda e: e.tensor_copy(sinb[:], sinT[:]))
    br = [C.sb("br", [128, 512]) for i in range(2)]; bi = [C.sb("bi", [128, 512]) for i in range(2)]
    brB = [C.buf("br") for _ in range(2)]; biB = [C.buf("bi") for _ in range(2)]
    tA = C.sb("tA", [128, 512]); tBq = C.sb("tBq", [128, 512]); tAB = C.buf("tA"); tBB = C.buf("tB")
    tC = {"dve": C.sb("tC", [128, 512], BF16), "pool": C.sb("tE", [128, 512])}
    tD = {"dve": C.sb("tD", [128, 512], BF16), "pool": C.sb("tF", [128, 512])}
    tCB = {"dve": C.buf("tC"), "pool": C.buf("tE")}; tDB = {"dve": C.buf("tD"), "pool": C.buf("tF")}
    zri = [C.sb("zri", [128, 4, 512]) for _ in range(2)]; zii = [C.sb("zii", [128, 4, 512]) for _ in range(2)]
    zriB = [[C.buf("zri") for _ in range(4)] for _ in range(2)]; ziiB = [[C.buf("zii") for _ in range(4)] for _ in range(2)]
    zr = C.sb("zr", [128, 4, 512]); zi = C.sb("zi", [128, 4, 512])
    zrB = [C.buf("zr") for _ in range(4)]; ziB = [C.buf("zi") for _ in range(4)]
    car_r = C.sb("car_r", [128, 4]); car_i = C.sb("car_i", [128, 4]); carB = C.buf("car")
    ct0 = C.sb("ct0", [128, 4]); ct1 = C.sb("ct1", [128, 4])
    P.op("dve", lambda e: e.memset(car_r[:], 0.0), [], [carB])
    P.op("dve", lambda e: e.memset(car_i[:], 0.0), [carB], [carB])
    xrb = [C.sb("xrb", [128, 512], BF16) for i in range(4)]; xib = [C.sb("xib", [128, 512], BF16) for i in range(4)]
    xrB = [C.buf("xr") for _ in range(4)]; xiB = [C.buf("xi") for _ in range(4)]
    yo = [C.sb("yo", [128, 512], BF16) for i in range(2)]; yoB = [C.buf("yo") for _ in range(2)]

    def v4(ap):
        return ap.rearrange("p (f j) -> p f j", f=4)

    def tb4(tab, st):
        return tab[:, st, :].unsqueeze(1).to_broadcast([128, 4, 128])

    def stage_a(blk):
        c0 = blk * 512
        zb = blk % 2
        ub = uB[c0 // PW]
        for st in range(4):
            i2 = st % 2
            b_r, b_rb = C.bank()
            b_i, b_ib = C.bank()
            P.op("pe", lambda e, st=st, b_r=b_r: e.matmul(b_r[:, :], BT[0][st][:, :], uT[:, c0:c0 + 512], start=True, stop=True), [kB, ub], [b_rb])
            P.op("pe", lambda e, st=st, b_i=b_i: e.matmul(b_i[:, :], BT[1][st][:, :], uT[:, c0:c0 + 512], start=True, stop=True), [kB, ub], [b_ib])
            P.op("act", lambda e, i2=i2, b_r=b_r: e.activation(out=br[i2][:, :], in_=b_r[:, :], func=AF.Copy), [b_rb], [brB[i2]])
            P.op("act", lambda e, i2=i2, b_i=b_i: e.activation(out=bi[i2][:, :], in_=b_i[:, :], func=AF.Copy), [b_ib], [biB[i2]])
            E = "pool"
            P.op(E, lambda e, st=st, i2=i2: e.tensor_tensor(out=v4(tA[:, :]), in0=v4(br[i2][:, :]), in1=tb4(cosT, st), op=ALU.mult), [brB[i2], kB], [tAB])
            P.op(E, lambda e, st=st, i2=i2: e.tensor_tensor(out=v4(tBq[:, :]), in0=v4(bi[i2][:, :]), in1=tb4(sinT, st), op=ALU.mult), [biB[i2], kB], [tBB])
            P.op(E, lambda e, st=st: e.tensor_tensor(out=zri[zb][:, st, :], in0=tA[:, :], in1=tBq[:, :], op=ALU.add), [tAB, tBB], [zriB[zb][st]])
            P.op(E, lambda e, st=st, i2=i2: e.tensor_tensor(out=v4(tA[:, :]), in0=v4(bi[i2][:, :]), in1=tb4(cosT, st), op=ALU.mult), [biB[i2], kB], [tAB])
            P.op(E, lambda e, st=st, i2=i2: e.tensor_tensor(out=v4(tBq[:, :]), in0=v4(br[i2][:, :]), in1=tb4(sinT, st), op=ALU.mult), [brB[i2], kB], [tBB])
            P.op(E, lambda e, st=st: e.tensor_tensor(out=zii[zb][:, st, :], in0=tA[:, :], in1=tBq[:, :], op=ALU.subtract), [tAB, tBB], [ziiB[zb][st]])

    def stage_b(blk):
        zb = blk % 2
        for f in range(4):
            fs = slice(f * 128, (f + 1) * 128)
            for st in range(4):
                P.op("dve", lambda e, st=st, fs=fs: e.tensor_tensor_scan(out=zr[:, st, fs], data0=rt[:, st, :], data1=zri[zb][:, st, fs], initial=car_r[:, st:st + 1], op0=ALU.mult, op1=ALU.add),
                     [kB, zriB[zb][st], carB], [zrB[st]])
                P.op("dve", lambda e, st=st, fs=fs: e.tensor_tensor_scan(out=zi[:, st, fs], data0=rt[:, st, :], data1=zii[zb][:, st, fs], initial=car_i[:, st:st + 1], op0=ALU.mult, op1=ALU.add),
                     [kB, ziiB[zb][st], carB], [ziB[st]])
            lc = f * 128 + 127
            P.op("dve", lambda e, lc=lc: e.tensor_tensor(out=ct0[:, :], in0=zi[:, :, lc], in1=Gi[:, :], op=ALU.mult), ziB + [kB], [carB])
            P.op("dve", lambda e, lc=lc: e.tensor_tensor(out=ct1[:, :], in0=zr[:, :, lc], in1=Gi[:, :], op=ALU.mult), zrB + [kB], [carB])
            P.op("dve", lambda e, lc=lc: e.tensor_tensor(out=car_r[:, :], in0=zr[:, :, lc], in1=Gr[:, :], op=ALU.mult), zrB + [kB], [carB])
            P.op("dve", lambda e: e.tensor_tensor(out=car_r[:, :], in0=car_r[:, :], in1=ct0[:, :], op=ALU.subtract), [carB], [carB])
            P.op("dve", lambda e, lc=lc: e.tensor_tensor(out=car_i[:, :], in0=zi[:, :, lc], in1=Gr[:, :], op=ALU.mult), ziB + [kB], [carB])
            P.op("dve", lambda e: e.tensor_tensor(out=car_i[:, :], in0=car_i[:, :], in1=ct1[:, :], op=ALU.add), [carB], [carB])

    def stage_c(blk):
        c0 = blk * 512
        yb, ybb = C.bank()
        for st in range(4):
            E = "dve" if st < 2 else "pool"
            TC, TD, TCB, TDB = tC[E], tD[E], tCB[E], tDB[E]
            P.op(E, lambda e, st=st, TC=TC, E=E: e.tensor_tensor(out=v4(TC[:, :]), in0=v4(zr[:, st, :]), in1=tb4(cosT if E == 'pool' else cosb, st), op=ALU.mult), [zrB[st], kB], [TCB])
            P.op(E, lambda e, st=st, TD=TD, E=E: e.tensor_tensor(out=v4(TD[:, :]), in0=v4(zi[:, st, :]), in1=tb4(sinT if E == 'pool' else sinb, st), op=ALU.mult), [ziB[st], kB], [TDB])
            P.op(E, lambda e, st=st, TC=TC, TD=TD: e.tensor_tensor(out=xrb[st][:, :], in0=TC[:, :], in1=TD[:, :], op=ALU.subtract), [TCB, TDB], [xrB[st]])
            P.op(E, lambda e, st=st, TC=TC, E=E: e.tensor_tensor(out=v4(TC[:, :]), in0=v4(zr[:, st, :]), in1=tb4(sinT if E == 'pool' else sinb, st), op=ALU.mult), [zrB[st], kB], [TCB])
            P.op(E, lambda e, st=st, TD=TD, E=E: e.tensor_tensor(out=v4(TD[:, :]), in0=v4(zi[:, st, :]), in1=tb4(cosT if E == 'pool' else cosb, st), op=ALU.mult), [ziB[st], kB], [TDB])
            P.op(E, lambda e, st=st, TC=TC, TD=TD: e.tensor_tensor(out=xib[st][:, :], in0=TC[:, :], in1=TD[:, :], op=ALU.add), [TCB, TDB], [xiB[st]])
            P.op("pe", lambda e, st=st, yb=yb: e.matmul(yb[:, :], CT[0][st][:, :], xrb[st][:, :], start=(st == 0), stop=False), [kB, xrB[st]], [ybb])
            P.op("pe", lambda e, st=st, yb=yb: e.matmul(yb[:, :], CT[1][st][:, :], xib[st][:, :], start=False, stop=(st == 3)), [kB, xiB[st]], [ybb])
        oi = blk % 2
        P.op("act", lambda e, oi=oi, yb=yb: e.activation(out=yo[oi][:, :], in_=yb[:, :], func=AF.Copy), [ybb], [yoB[oi]])
        P.dma("sp", lambda e, oi=oi: e.dma_start(out=ycs[:, 2 + c0:2 + c0 + 512], in_=yo[oi][:, :]), [yoB[oi]], [ycsB])

    stage_a(0)
    for blk in range(nblk):
        if blk + 1 < nblk:
            stage_a(blk + 1)
        stage_b(blk)
        stage_c(blk)
    C.release(m0)


def s5_consts():
    return dict(identf=np.eye(128, dtype=np.float32), jidx=np.ascontiguousarray(np.broadcast_to(np.arange(128, dtype=np.float32), (128, 128))))


def emit_norm(C, io):
    P = C.P
    m0 = C.mark()
    xs = io["xs"]; xsB = io["xsB"]
    hn_d = io["hn_d"]; hnB_d = io["hn_dB"]
    ones, onesB, eps, epsB = C.K
    NRM = Norm(C, ones, onesB, eps, epsB)
    g = C.sb("g", [128, 8]); gB = C.buf("g")
    load_f32(C, g[:], io["g"], gB, "pool")
    hnt = [C.sb("hnt", [128, 8, 512], BF16) for i in range(2)]; hnB = [C.buf("hn") for _ in range(2)]
    rstd = C.sb("rstd", [128, 512]); rstdB = C.buf("rstd")
    for tt in range(NTT):
        i = tt % 2
        c0 = 2 + tt * 512
        NRM.rstd(lambda k, c0=c0: xs[:, k, c0:c0 + 512], [xsB[tt]], 512, rstd[:, :], rstdB)
        for k in range(8):
            P.op("dve", lambda e, k=k, i=i, c0=c0: e.scalar_tensor_tensor(out=hnt[i][:, k, :], in0=xs[:, k, c0:c0 + 512], scalar=g[:, k:k + 1], in1=rstd[:, :], op0=ALU.mult, op1=ALU.mult),
                 [xsB[tt], gB, rstdB], [hnB[i]])
        P.dma("pool", lambda e, i=i, tt=tt: e.dma_start(out=hn_d.rearrange("(k p) t -> p k t", p=128)[:, :, tt * 512:(tt + 1) * 512], in_=hnt[i][:, :, :]), [hnB[i]], [hnB_d])
    C.release(m0)


I32 = mybir.dt.int32
DEPTH = 4


def allgather(C, src, srcB, dst, dstB):
    C.P.coll(lambda e: e.collective_compute("AllGather", ALU.bypass, replica_groups=[list(range(NCORES))], ins=[src], outs=[dst]), [srcB], [dstB])


def build_fused():
    C = Ctx()
    P = C.P
    TH = TPC + 2
    xin = C.din("xin", [128, 8, TPC]); pT = C.din("pT", [DEPTH, 128, 2, TPC]); hmask = C.din("hmask", [128, 1])
    cid = C.din("cid", [1, 1], I32); pid = C.din("pid", [1, 1], I32)
    gmix = C.din("gmixc", [DEPTH, 128, 8]); gffn = C.din("gffnc", [DEPTH, 128, 8])
    win = C.din("win", [2, D, 2568]); vgain = C.din("vgain", [2, 128, 512]); wsT = C.din("wsT", [2, 128, 4, 128]); tri = C.din("tri", [128, 128])
    bsb = C.din("bsb", [2, 128, 512]); qg = C.din("qg", [2, 64, 1]); kg = C.din("kg", [2, 64, 1]); bfg = C.din("bfg", [2, 8, 1])
    wout = C.din("wout", [2, D, D])
    maskneg = C.din("maskneg", [128, 4, 512], BF16); ident = C.din("ident", [128, 128], BF16); mcols = C.din("mcols", [128, 4])
    are = C.din("are", [2, 128, 4]); aim = C.din("aim", [2, 128, 4]); ldt = C.din("ldt", [2, 128, 4])
    bre = C.din("bre", [2, 128, 4, 16]); bim = C.din("bim", [2, 128, 4, 16]); creT = C.din("creT", [2, 128, 4, 16]); cimT = C.din("cimT", [2, 128, 4, 16])
    identf = C.din("identf", [128, 128]); jidx = C.din("jidx", [128, 128])
    dvec = C.din("dvec", [2, 128, 8]); wglu = C.din("wglu", [2, D, 2 * D])
    wup = C.din("wup", [DEPTH, D, 2 * DFF]); cw = C.din("cw", [DEPTH, 128, 3, 2 * NJ]); cb = C.din("cb", [DEPTH, 128, 2 * NJ])
    wdn = C.din("wdn", [DEPTH, DFF, D]); wgate = C.din("wgate", [DEPTH, D, D]); wproj = C.din("wproj", [DEPTH, 256, D])
    xout = C.dout("xout", [128, 8, TPC])
    hn_d = C.dint("hn_d", [D, TPC], BF16); hng = C.dint("hng", [NCORES * D, TPC], BF16)
    ya_d = C.dint("ya_d", [128, 4, TPC], BF16)
    qks = C.dint("qks", [8 * 128, TPC], BF16); qkg = C.dint("qkg", [NCORES * 8 * 128, TPC], BF16)
    Vs = C.dint("Vs", [8 * TPC, 64], BF16); Vg = C.dint("Vg", [NCORES * 8 * TPC, 64], BF16)
    lss = C.dint("lss", [8, TPC]); lsg = C.dint("lsg", [NCORES * 8, TPC])
    hsend = C.dint("hsend", [128, 24]); hrecv = C.dint("hrecv", [NCORES * 128, 24])
    osend = C.dint("osend", [64, SEQ + 2], BF16); og = C.dint("og", [NCORES * 64, SEQ + 2], BF16)
    ycs = C.dint("ycs", [128, SEQ + 2], BF16); ycg = C.dint("ycg", [NCORES * 128, SEQ + 2], BF16)
    B = {k: C.buf(k) for k in ("hn_d", "hng", "ya_d", "qks", "qkg", "Vs", "Vg", "lss", "lsg", "hsend", "hrecv", "osend", "og", "ycs", "ycg", "out")}

    P.eng_init = {"sp": lambda h: {"cid": h.value_load(cid)}, "act": lambda h: {"cid": h.value_load(cid), "pid": h.value_load(pid)}}
    qk_mine = C.dint("qk_mine", [128, SEQ], BF16); V_mine = C.dint("V_mine", [SEQ, 64], BF16); ls_mine = C.dint("ls_mine", [1, SEQ])
    B["mine"] = C.buf("mine")
    C.K = consts(C)
    xs = C.sb("xs", [128, 8, TH], F32)
    xsB = [C.buf("xs") for _ in range(NTT)]
    xsH = C.buf("xsH")
    for k in range(8):
        for tt in range(NTT):
            c0 = 2 + tt * 512
            P.dma("sp", lambda e, k=k, c0=c0, tt=tt: e.dma_start(out=xs[:, k, c0:c0 + 512], in_=xin[:, k, tt * 512:(tt + 1) * 512]), [], [xsB[tt]])
    base = dict(xs=xs, xsB=xsB, xsH=xsH, outB=B["out"], hn_d=hn_d, hn_dB=B["hn_d"])
    emit_norm(C, dict(base, g=gmix[0]))
    for l in range(DEPTH):
        if l % 2 == 0:
            e_ = l // 2
            emit_pre_even(C, dict(base, win=win[e_], vgain=vgain[e_], wsT=wsT[e_], tri=tri, bsb=bsb[e_], qg=qg[e_], kg=kg[e_], bfg=bfg[e_],
                                  ya_d=ya_d, ya_dB=B["ya_d"], qks=qks, qksB=B["qks"], Vs=Vs, VsB=B["Vs"], lss=lss, lssB=B["lss"], hsend=hsend, hsendB=B["hsend"],
                                  qkg=qkg, qkgB=B["qkg"], Vg=Vg, VgB=B["Vg"], lsg=lsg, lsgB=B["lsg"], hrecv=hrecv, hrecvB=B["hrecv"],
                                  qk_mine=qk_mine, V_mine=V_mine, ls_mine=ls_mine, mineB=B["mine"]))
            emit_attn(C, dict(qk_mine=qk_mine, V_mine=V_mine, ls_mine=ls_mine, mineB=B["mine"], qkg=qkg, qkgB=B["qkg"], Vg=Vg, VgB=B["Vg"], lsg=lsg, lsgB=B["lsg"], osend=osend, osendB=B["osend"],
                              maskneg=maskneg, ident=ident, mcols=mcols))
            allgather(C, osend, B["osend"], og, B["og"])
            mode = "even"
            extra = dict(wmix=wout[e_], ya_d=ya_d, ya_dB=B["ya_d"], og=og, ogB=B["og"])
        else:
            o_ = l // 2
            m0 = C.mark()
            hs = C.sb("hs", [128, 24]); hsB = C.buf("hs")
            P.op("dve", lambda e, hs=hs: e.memset(hs[:, :], 0.0), [], [hsB])
            P.op("dve", lambda e, hs=hs: e.tensor_copy(hs[:, 0:16].rearrange("p (k t) -> p k t", k=8), xs[:, :, TPC:TPC + 2]), [xsB[NTT - 1], hsB], [hsB])
            P.dma("pool", lambda e, hs=hs: e.dma_start(out=hsend, in_=hs[:, :]), [hsB], [B["hsend"]])
            C.release(m0)
            allgather(C, hsend, B["hsend"], hrecv, B["hrecv"])
            allgather(C, hn_d, B["hn_d"], hng, B["hng"])
            emit_s5(C, dict(hng=hng, hngB=B["hng"], ycs=ycs, ycsB=B["ycs"], are=are[o_], aim=aim[o_], ldt=ldt[o_], bre=bre[o_], bim=bim[o_],
                            creT=creT[o_], cimT=cimT[o_], identf=identf, jidx=jidx))
            allgather(C, ycs, B["ycs"], ycg, B["ycg"])
            mode = "odd"
            extra = dict(wmix=wglu[o_], gmix=gmix[l], dvec=dvec[o_], ycg=ycg, ycgB=B["ycg"])
        last = (l == DEPTH - 1)
        io = dict(base, gffn=gffn[l], wup=wup[l], cw=cw[l], cb=cb[l], wdn=wdn[l], wgate=wgate[l], wproj=wproj[l], pT=pT[l],
                  gnext=gmix[(l + 1) % DEPTH], hmask=hmask, hrecv=hrecv, hrecvB=B["hrecv"], **extra)
        if last:
            io["hn_d"] = None
            io["xout"] = xout
        emit_post(C, mode, io)
    return C.finish()


def _col(v):
    return np.ascontiguousarray(np.asarray(v, np.float32).reshape(-1, 128).T)


def _f3(a):
    F, T = a.shape
    return np.ascontiguousarray(a.reshape(F // 128, 128, T).transpose(1, 0, 2))


def _s5lay(a):
    a = np.asarray(a, np.float32)
    a = a.reshape(4, 2, 64, *a.shape[2:])
    a = np.moveaxis(a, 0, 2)
    return np.ascontiguousarray(a.reshape(128, 4, *a.shape[3:]))


def kernel(x, p, norm_mix, norm_ffn, ev_w_in, ev_b_fgate, ev_q_norm, ev_k_norm, ev_v_norm,
           ev_w_spatial, ev_b_spatial, ev_w_out, od_a_re, od_a_im, od_log_dt, od_b_re, od_b_im,
           od_c_re, od_c_im, od_d, od_w_glu, ffn_w_up, ffn_conv_w, ffn_conv_b, ffn_w_down,
           ple_w_proj, ple_w_gate):
    A = lambda a: np.ascontiguousarray(np.asarray(a, np.float32))
    x = A(x); p = A(p)
    XT = np.ascontiguousarray(x[0].T)
    ac = attn_consts()
    sc = s5_consts()
    com = dict(
        gmixc=np.stack([_col(norm_mix[l]) for l in range(DEPTH)]), gffnc=np.stack([_col(norm_ffn[l]) for l in range(DEPTH)]),
        win=A(ev_w_in), vgain=np.stack([np.broadcast_to(A(ev_v_norm[e]), (128, 512)) for e in range(2)]).copy(),
        wsT=np.stack([A(ev_w_spatial[e]).transpose(2, 0, 1) for e in range(2)]).copy(), tri=np.triu(np.ones((128, 128), np.float32)),
        bsb=np.stack([np.broadcast_to(A(ev_b_spatial[e]).reshape(-1), (128, 512)) for e in range(2)]).copy(),
        qg=A(ev_q_norm).reshape(2, 64, 1), kg=A(ev_k_norm).reshape(2, 64, 1), bfg=A(ev_b_fgate).reshape(2, 8, 1), wout=A(ev_w_out),
        dvec=np.stack([_col(od_d[o]) for o in range(2)]), wglu=A(od_w_glu),
        wup=A(ffn_w_up), cw=np.stack([A(ffn_conv_w[l]).reshape(3, 2 * NJ, 128).transpose(2, 0, 1) for l in range(DEPTH)]).copy(),
        cb=np.stack([_col(ffn_conv_b[l]) for l in range(DEPTH)]), wdn=A(ffn_w_down), wgate=A(ple_w_gate), wproj=A(ple_w_proj),
        **ac, **sc)
    maps = []
    for c in range(NCORES):
        gs = slice(8 * c, 8 * c + 8)
        m = dict(com)
        m.update(
            xin=_f3(XT[:, c * TPC:(c + 1) * TPC]),
            pT=np.stack([p[l, 0, c * TPC:(c + 1) * TPC, :].T.reshape(2, 128, TPC).transpose(1, 0, 2) for l in range(DEPTH)]).copy(),
            hmask=np.full((128, 1), 0.0 if c == 0 else 1.0, np.float32),
            cid=np.array([[c]], np.int32), pid=np.array([[(c - 1) % NCORES]], np.int32),
            are=np.stack([_s5lay(od_a_re[o][gs]) for o in range(2)]), aim=np.stack([_s5lay(od_a_im[o][gs]) for o in range(2)]),
            ldt=np.stack([_s5lay(np.broadcast_to(A(od_log_dt[o][gs])[:, None], (8, 64))) for o in range(2)]),
            bre=np.stack([_s5lay(od_b_re[o][gs]) for o in range(2)]), bim=np.stack([_s5lay(od_b_im[o][gs]) for o in range(2)]),
            creT=np.stack([_s5lay(A(od_c_re[o][gs]).transpose(0, 2, 1)) for o in range(2)]),
            cimT=np.stack([_s5lay(A(od_c_im[o][gs]).transpose(0, 2, 1)) for o in range(2)]))
        maps.append(m)
    nc = build_fused()
    res = run_bass_kernel_spmd(nc, maps, core_ids=list(range(NCORES)))
    XT = np.concatenate([res.results[c]["xout"].transpose(1, 0, 2).reshape(D, TPC) for c in range(NCORES)], axis=1)
    return np.ascontiguousarray(XT.T)[None].astype(np.float32)
```
